# Optimizing a Trainium2 kernel written in Bass

```python
import math
import jax, jax.numpy as jnp
from jax import lax
import numpy as np

D_MODEL = 1024
BATCH = 8
SEQ = 4096
DEPTH = 4

DA_HEADS = 4
DA_HEAD_DIM = 64
DA_V_DIM = 2 * DA_HEAD_DIM
DA_QK = 2 * DA_HEADS * DA_HEAD_DIM
DA_WIDTH = DA_HEADS * DA_V_DIM
DIL_GROUPS = ((128, 1), (512, 4), (2048, 16))
DB_HEADS = 8
DB_HEAD_DIM = 64
DB_WIDTH = DB_HEADS * DB_HEAD_DIM
N_BRANCH = 2
FFN_HIDDEN = -(-8 * D_MODEL // (3 * 256)) * 256
ROPE_THETA = 10000.0
Q_BLOCK = 128
NORM_EPS = 1e-6
SUBLN_EPS = 1e-5
NEG_INF = -1e30
IN_SIZES = [DA_QK, DA_QK, DA_WIDTH] + [DB_WIDTH] * (3 * len(DIL_GROUPS)) + [N_BRANCH * D_MODEL]
IN_COLS = sum(IN_SIZES)

kernel_name = "hybrid_diffattn_dilated_swiglu_encoder"


def rms_norm(x, g, eps):
    xf = x.astype(jnp.float32)
    y = xf * lax.rsqrt(jnp.mean(xf * xf, axis=-1, keepdims=True) + eps)
    return (y * g.astype(jnp.float32)).astype(x.dtype)


def rope_tables(seq, dim):
    inv = 1.0 / (ROPE_THETA ** (jnp.arange(0, dim, 2, dtype=jnp.float32) / dim))
    ang = jnp.arange(seq, dtype=jnp.float32)[:, None] * inv[None, :]
    return jnp.cos(ang), jnp.sin(ang)


def apply_rope(t, cos, sin):
    half = t.shape[-1] // 2
    tf = t.astype(jnp.float32)
    t1, t2 = tf[..., :half], tf[..., half:]
    c, s = cos[None, :, None, :], sin[None, :, None, :]
    return jnp.concatenate([t1 * c - t2 * s, t2 * c + t1 * s], axis=-1).astype(t.dtype)


def differential_attention(q, k, v, lam_full, subln_g, lambda_init):
    B, S, H2, Dh = q.shape
    H = H2 // 2
    scale = Dh ** -0.5
    qb = q.reshape(B, S // Q_BLOCK, Q_BLOCK, H, 2, Dh).transpose(1, 0, 2, 3, 4, 5)
    kk = k.reshape(B, S, H, 2, Dh)

    def block(qblk):
        s = jnp.einsum('bqhcd,bkhcd->bhcqk', qblk, kk, preferred_element_type=jnp.float32) * scale
        p = jax.nn.softmax(s, axis=-1)
        a = p[:, :, 0] - lam_full * p[:, :, 1]
        return jnp.einsum('bhqk,bkhe->bqhe', a.astype(v.dtype), v)

    o = lax.map(block, qb)
    o = o.transpose(1, 0, 2, 3, 4).reshape(B, S, H, 2 * Dh)
    o = rms_norm(o, subln_g, SUBLN_EPS) * (1.0 - lambda_init)
    return o.reshape(B, S, H * 2 * Dh)


def dilated_window_attention(q, k, v, dilation, n_side):
    B, S, H, Dh = q.shape
    blk = n_side
    L = -(-S // (dilation * blk)) * blk
    pad = L * dilation - S
    nb = L // blk

    def split(t):
        t = jnp.pad(t, ((0, 0), (0, pad), (0, 0), (0, 0)))
        t = t.reshape(B, L, dilation, H, Dh).transpose(0, 2, 1, 3, 4)
        return t.reshape(B, dilation, nb, blk, H, Dh)

    def neighbours(t):
        tp = jnp.pad(t, ((0, 0), (0, 0), (1, 1), (0, 0), (0, 0), (0, 0)))
        return jnp.concatenate([tp[:, :, :-2], tp[:, :, 1:-1], tp[:, :, 2:]], axis=3)

    qb = split(q)
    kn = neighbours(split(k))
    vn = neighbours(split(v))
    r = jnp.arange(dilation)[:, None, None, None]
    n = jnp.arange(nb)[None, :, None, None]
    i = jnp.arange(blk)[None, None, :, None]
    j = jnp.arange(3 * blk)[None, None, None, :]
    tq = n * blk + i
    tk = n * blk + j - blk
    mask = (jnp.abs(tk - tq) <= n_side) & (tk >= 0) & (tk * dilation + r < S)
    s = jnp.einsum('brnqhd,brnkhd->brnhqk', qb, kn, preferred_element_type=jnp.float32) * (Dh ** -0.5)
    s = jnp.where(mask[None, :, :, None, :, :], s, NEG_INF)
    m = jnp.max(s, axis=-1, keepdims=True)
    e = jnp.exp(s - m)
    den = jnp.sum(e, axis=-1)
    o = jnp.einsum('brnhqk,brnkhd->brnqhd', e.astype(vn.dtype), vn, preferred_element_type=jnp.float32)
    o = o / jnp.swapaxes(den, -1, -2)[..., None]
    lse = jnp.swapaxes(m[..., 0] + jnp.log(den), -1, -2)

    def merge(t):
        rest = t.shape[4:]
        t = t.reshape((B, dilation, L) + rest)
        t = jnp.moveaxis(t, 1, 2).reshape((B, L * dilation) + rest)
        return t[:, :S]

    return merge(o), merge(lse)


def hybrid_mixer(h, w_in, w_a, w_b, w_out, lam, subln_g, cos, sin, lambda_init):
    B, S, _ = h.shape
    z = h @ w_in
    idx = np.cumsum(IN_SIZES)[:-1].tolist()
    parts = jnp.split(z, idx, axis=-1)
    qa = apply_rope(parts[0].reshape(B, S, 2 * DA_HEADS, DA_HEAD_DIM), cos, sin)
    ka = apply_rope(parts[1].reshape(B, S, 2 * DA_HEADS, DA_HEAD_DIM), cos, sin)
    va = parts[2].reshape(B, S, DA_HEADS, DA_V_DIM)
    lamf = lam.astype(jnp.float32)
    lam_full = (jnp.exp(jnp.sum(lamf[0] * lamf[1])) - jnp.exp(jnp.sum(lamf[2] * lamf[3]))
                + lambda_init)
    oa = differential_attention(qa, ka, va, lam_full, subln_g, lambda_init)
    outs, lses = [], []
    for g, (window, dilation) in enumerate(DIL_GROUPS):
        base = 3 + 3 * g
        qg = apply_rope(parts[base].reshape(B, S, DB_HEADS, DB_HEAD_DIM), cos, sin)
        kg = apply_rope(parts[base + 1].reshape(B, S, DB_HEADS, DB_HEAD_DIM), cos, sin)
        vg = parts[base + 2].reshape(B, S, DB_HEADS, DB_HEAD_DIM)
        og, lg = dilated_window_attention(qg, kg, vg, dilation, (window // 2) // dilation)
        outs.append(og)
        lses.append(lg)
    wts = jax.nn.softmax(jnp.stack(lses), axis=0)
    ob = jnp.einsum('gbsh,gbshd->bshd', wts, jnp.stack(outs)).astype(h.dtype).reshape(B, S, DB_WIDTH)
    gates = jax.nn.sigmoid(parts[-1].astype(jnp.float32)).reshape(B, S, N_BRANCH, D_MODEL)
    merged = gates[:, :, 0] * (oa @ w_a).astype(jnp.float32) + gates[:, :, 1] * (ob @ w_b).astype(jnp.float32)
    return merged.astype(h.dtype) @ w_out


def swiglu(h, w_gate_up, w_down):
    gu = h @ w_gate_up
    g, u = jnp.split(gu, 2, axis=-1)
    return (jax.nn.silu(g) * u) @ w_down


def setup_inputs(seed: int = 0) -> dict:
    key = jax.random.key(seed)
    ks = jax.random.split(key, 14)
    f32 = jnp.float32

    def nrm(k, shape, scale):
        return jax.random.normal(k, shape, f32) * scale

    def gain(k, shape):
        return 1.0 + 0.05 * jax.random.normal(k, shape, f32)

    return {
        "x": jax.random.normal(ks[0], (BATCH, SEQ, D_MODEL), f32),
        "w_in": nrm(ks[1], (DEPTH, D_MODEL, IN_COLS), D_MODEL ** -0.5),
        "w_a": nrm(ks[2], (DEPTH, DA_WIDTH, D_MODEL), DA_WIDTH ** -0.5),
        "w_b": nrm(ks[3], (DEPTH, DB_WIDTH, D_MODEL), DB_WIDTH ** -0.5),
        "w_out": nrm(ks[4], (DEPTH, D_MODEL, D_MODEL), D_MODEL ** -0.5),
        "lam": nrm(ks[5], (DEPTH, 4, DA_HEAD_DIM), 0.1),
        "subln_g": gain(ks[6], (DEPTH, DA_V_DIM)),
        "norm_mix_pre": gain(ks[7], (DEPTH, D_MODEL)),
        "norm_mix_post": gain(ks[8], (DEPTH, D_MODEL)),
        "w_gate_up": nrm(ks[9], (DEPTH, D_MODEL, 2 * FFN_HIDDEN), D_MODEL ** -0.5),
        "w_down": nrm(ks[10], (DEPTH, FFN_HIDDEN, D_MODEL), FFN_HIDDEN ** -0.5),
        "norm_ffn_pre": gain(ks[11], (DEPTH, D_MODEL)),
        "norm_ffn_post": gain(ks[12], (DEPTH, D_MODEL)),
    }


def reference(x, w_in, w_a, w_b, w_out, lam, subln_g, norm_mix_pre, norm_mix_post,
              w_gate_up, w_down, norm_ffn_pre, norm_ffn_post):
    S = x.shape[1]
    cos, sin = rope_tables(S, DA_HEAD_DIM)
    for l in range(DEPTH):
        lambda_init = 0.8 - 0.6 * math.exp(-0.3 * l)
        h = rms_norm(x, norm_mix_pre[l], NORM_EPS)
        m = hybrid_mixer(h, w_in[l], w_a[l], w_b[l], w_out[l], lam[l], subln_g[l], cos, sin, lambda_init)
        x = x + rms_norm(m, norm_mix_post[l], NORM_EPS)
        h = rms_norm(x, norm_ffn_pre[l], NORM_EPS)
        f = swiglu(h, w_gate_up[l], w_down[l])
        x = x + rms_norm(f, norm_ffn_post[l], NORM_EPS)
    return x
```

```python
import contextlib
import math
import numpy as np
import concourse.bass as bass
import concourse.mybir as mybir
from concourse.bass_utils import run_bass_kernel_spmd

F32 = mybir.dt.float32
BF16 = mybir.dt.bfloat16
U8 = mybir.dt.uint8
ALU = mybir.AluOpType
AF = mybir.ActivationFunctionType
AX = mybir.AxisListType

S_ = 4096
D_ = 1024
DEPTH = 4
TC = 512
NTC = S_ // TC
FF = 2816
NFC = FF // 128
IN_COLS = 8192
DIL = (1, 4, 16)
NORM_EPS = 1e-6
SUBLN_EPS = 1e-5
MASK_NEG = -30000.0
N_CORES = 8
RING = 8
RINGS = {"sp": 8, "pool": 1}


class Buf:
    __slots__ = ("w", "r", "excl")

    def __init__(self, excl=False):
        self.w = None
        self.r = {}
        self.excl = excl


class Op:
    __slots__ = ("eng", "fn", "deps", "seq", "sig", "sigval", "dma", "dman", "waits")


class Sched:
    ENG = ("pe", "act", "dve", "pool", "sp")

    def __init__(self):
        self.prog = {e: [] for e in self.ENG}
        self.ndma = {e: 0 for e in self.ENG}
        self.dma_since_bar = []

    def add(self, eng, fn, reads=(), writes=(), dma=False, extra=()):
        op = Op()
        op.eng = eng
        op.fn = fn
        op.dma = dma
        op.sig = False
        op.sigval = 0
        op.dman = -1
        deps = {}
        for b in reads:
            if b.w is not None:
                deps[id(b.w)] = b.w
            if b.excl:
                for r in b.r.values():
                    if r.eng != eng:
                        deps[id(r)] = r
        for b in writes:
            if b.w is not None:
                deps[id(b.w)] = b.w
            for r in b.r.values():
                deps[id(r)] = r
        for d in extra:
            deps[id(d)] = d
        op.deps = list(deps.values())
        op.seq = len(self.prog[eng])
        self.prog[eng].append(op)
        if dma:
            op.dman = self.ndma[eng]
            self.ndma[eng] += 1
            self.dma_since_bar.append(op)
        for b in reads:
            b.r[(eng, op.seq) if dma else eng] = op
        for b in writes:
            b.w = op
            b.r = {}
        return op

    def barrier(self):
        last = []
        for e in self.ENG:
            for op in reversed(self.prog[e]):
                if op.fn is not None and not op.dma:
                    last.append(op)
                    break
        bar = self.add("sp", self.bar_fn, extra=last + [d for d in self.dma_since_bar if d.eng == "sp"], dma=True)
        self.dma_since_bar = []
        for e in self.ENG:
            if e != "sp":
                self.add(e, None, extra=[bar])
        return bar

    def emit(self, nc, block, sems, rings):
        for eng in self.ENG:
            waited = {}
            waited_dma = set()
            for op in self.prog[eng]:
                op.waits = []
                best = {}
                for d in op.deps:
                    if d.dma:
                        if id(d) not in waited_dma:
                            waited_dma.add(id(d))
                            op.waits.append(d)
                    else:
                        if d.eng == "pe" and eng == "pe":
                            continue
                        if d.eng not in best or best[d.eng].seq < d.seq:
                            best[d.eng] = d
                for pe_, d in best.items():
                    if waited.get(pe_, -1) >= d.seq:
                        continue
                    waited[pe_] = d.seq
                    d.sig = True
                    op.waits.append(d)
        for eng in self.ENG:
            cnt = 0
            for op in self.prog[eng]:
                if not op.dma and op.sig:
                    cnt += 1
                    op.sigval = cnt
        hook = {"pe": block.tensor, "act": block.scalar, "dve": block.vector,
                "pool": block.gpsimd, "sp": block.sync}

        def semval(d):
            if d.dma:
                return rings[d.eng][d.dman % RINGS[d.eng]], 16 * (d.dman // RINGS[d.eng] + 1)
            return sems[d.eng], d.sigval

        for eng in self.ENG:
            ops = self.prog[eng]

            def body(e, ops=ops, eng=eng):
                for op in ops:
                    for d in op.waits:
                        s, v = semval(d)
                        e.wait_ge(s, v)
                    if op.dma:
                        n = op.dman
                        R_ = RINGS[eng]
                        if n >= R_:
                            e.wait_ge(rings[eng][n % R_], 16 * (n // R_))
                        ins = op.fn(e)
                        ins.then_inc(rings[eng][n % R_], 16)
                    elif op.fn is not None:
                        ins = op.fn(e)
                        if op.sig:
                            ins.then_inc(sems[eng], 1)
                    else:
                        assert not op.sig

            hook[eng](body)


class T:
    def __init__(self, ap, nsub=0):
        self.ap = ap
        self.b = Buf()
        self.s = [Buf() for _ in range(nsub)]


class Arena:
    def __init__(self, base_ap, nbytes):
        self.base = base_ap
        self.nbytes = nbytes
        self.off = 0

    def reset(self):
        self.off = 0

    def alloc(self, shape, dtype, nsub=0):
        esz = 4 if dtype == F32 else 2
        n = int(np.prod(shape[1:]))
        nb = n * esz
        nb_al = (nb + 63) // 64 * 64
        assert self.off + nb_al <= self.nbytes, (self.off, nb_al, self.nbytes)
        ap = self.base[:, self.off:self.off + nb].bitcast(dtype)
        self.off += nb_al
        if len(shape) == 3:
            ap = ap.rearrange("p (a b) -> p a b", a=shape[1])
        elif len(shape) == 4:
            ap = ap.rearrange("p (a b c) -> p a b c", a=shape[1], b=shape[2])
        return T(ap, nsub)


def lambda_init(l):
    return 0.8 - 0.6 * math.exp(-0.3 * l)


def build(n_layers=DEPTH, dbg=None, upto=None, nheads=(4, 4), WD=DEPTH):
    nc = bass.Bass("TRN2", target_bir_lowering=False)
    S = Sched()

    def dram(name, shape, dt, kind):
        return nc.dram_tensor(name, list(shape), dt, kind=kind).ap()

    x_d = dram("x", [S_, D_], F32, "ExternalInput")
    w_in_d = dram("w_in", [WD, D_, IN_COLS], F32, "ExternalInput")
    w_a_d = dram("w_a", [WD, 512, D_], F32, "ExternalInput")
    w_b_d = dram("w_b", [WD, 512, D_], F32, "ExternalInput")
    w_out_d = dram("w_out", [WD, D_, D_], F32, "ExternalInput")
    lam_d = dram("lam", [1, DEPTH * 4 * 64], F32, "ExternalInput")
    subln_d = dram("subln_g", [DEPTH, 128], F32, "ExternalInput")
    norms_d = dram("norms", [4 * DEPTH * 8, 128], F32, "ExternalInput")
    w_gu_d = dram("w_gate_up", [WD, D_, 2 * FF], F32, "ExternalInput")
    w_dn_d = dram("w_down", [WD, FF, D_], F32, "ExternalInput")
    cos_d = dram("cos2", [128, S_], F32, "ExternalInput")
    sin_d = dram("ssin", [128, S_], F32, "ExternalInput")
    cst_d = dram("consts", [128, 640], F32, "ExternalInput")
    out_d = dram("out", [S_, D_], F32, "ExternalOutput")

    XT_d = dram("XT", [128, 8, S_], F32, "Internal")
    HT_d = dram("HT", [128, 8, S_], BF16, "Internal")
    OA_d = dram("OAT", [128, 4, S_], BF16, "Internal")
    OB_d = dram("OBT", [128, 4, S_], BF16, "Internal")
    wb_in = dram("wb_in", [DEPTH, D_, IN_COLS], BF16, "Internal")
    wb_a = dram("wb_a", [DEPTH, 512, D_], BF16, "Internal")
    wb_b = dram("wb_b", [DEPTH, 512, D_], BF16, "Internal")
    wb_out = dram("wb_out", [DEPTH, D_, D_], BF16, "Internal")
    wb_gu = dram("wb_gu", [DEPTH, D_, 2 * FF], BF16, "Internal")
    wb_dn = dram("wb_dn", [DEPTH, FF, D_], BF16, "Internal")
    bar_d = dram("bar_scratch", [2, 64], F32, "Internal")
    S.bar_fn = lambda e: e.dma_start(out=bar_d[1:2, :], in_=cst_d[0:1, 0:64])
    dbg_d = {}
    if dbg:
        for name, shape, dt in dbg:
            dbg_d[name] = dram("dbg_" + name, shape, dt, "ExternalOutput")

    XT_b = [Buf() for _ in range(NTC)]
    HT_b = [Buf() for _ in range(NTC)]
    OA_b = [[Buf() for _ in range(NTC)] for _ in range(4)]
    OB_b = [[Buf() for _ in range(NTC)] for _ in range(4)]
    WB_b = {k: [Buf() for _ in range(DEPTH)] for k in ("in", "a", "b", "out", "gu", "dn")}
    out_bs = [Buf() for _ in range(NTC + 4)]

    es = contextlib.ExitStack()
    with es:
        sems = {e: es.enter_context(nc.semaphore("sem_" + e)) for e in Sched.ENG}
        rings = {e: [es.enter_context(nc.semaphore("ring_%s%d" % (e, i))) for i in range(RING)]
                 for e in ("sp", "pool")}
        SHARED = 50 * 1024
        OVL = 140 * 1024
        sh_t = es.enter_context(nc.sbuf_tensor("shared", [128, SHARED], U8))
        ov_t = es.enter_context(nc.sbuf_tensor("overlay", [128, OVL], U8))
        shA = Arena(sh_t[:], SHARED)
        ovA = Arena(ov_t[:], OVL)
        psum = []
        for i in range(8):
            pt_ = es.enter_context(nc.psum_tensor("ps%d" % i, [128, 512], F32))
            psum.append(T(pt_[:]))
            psum[-1].b.excl = True
        block = es.enter_context(nc.Block())

        cst_f = shA.alloc([128, 640], F32)
        ident_f = cst_f.ap[:, 0:128]
        cst_bf = shA.alloc([128, 640], BF16)
        ident_bf = cst_bf.ap[:, 0:128]
        perm_bf = cst_bf.ap[:, 128:256]
        mask_bf = cst_bf.ap[:, 256:512]
        ones_bf = cst_bf.ap[:, 512:640]
        gains = shA.alloc([128, 128], F32)
        gsub = shA.alloc([128, 4], F32)
        neglam = shA.alloc([128, 4], F32)
        hbuf = [shA.alloc([128, 8, TC], BF16) for _ in range(2)]
        tf = [shA.alloc([128, TC], F32) for _ in range(8)]
        tb = [shA.alloc([128, TC], BF16) for _ in range(4)]
        obuf = [shA.alloc([128, TC], BF16) for _ in range(2)]
        rsr = [shA.alloc([128, TC], F32) for _ in range(2)]
        rr = {"tf": 0, "tb": 0, "ob": 0, "hb": 0, "rs": 0}

        def rot(lst, key):
            t = lst[rr[key] % len(lst)]
            rr[key] += 1
            return t

        def gain(ty, l, c):
            i = (ty * DEPTH + l) * 8 + c
            return gains.ap[:, i:i + 1]

        def mm(out, lhsT, rhs, start, stop, reads, writes):
            S.add("pe", lambda e: e.matmul(out, lhsT=lhsT, rhs=rhs, start=start, stop=stop), reads, writes)

        def tr(out, in_, ident, reads, writes):
            S.add("pe", lambda e: e.transpose(out=out, in_=in_, identity=ident), reads, writes)

        def act(out, in_, func, reads, writes, scale=None, bias=None):
            kw = {}
            if scale is not None:
                kw["scale"] = scale
            if bias is not None:
                kw["bias"] = bias
            S.add("act", lambda e: e.activation(out=out, in_=in_, func=func, **kw), reads, writes)

        def tt(eng, out, in0, in1, op, reads, writes):
            S.add(eng, lambda e: e.tensor_tensor(out=out, in0=in0, in1=in1, op=op), reads, writes)

        def stt(out, in0, scalar, in1, op0, op1, reads, writes):
            S.add("dve", lambda e: e.scalar_tensor_tensor(out=out, in0=in0, scalar=scalar, in1=in1, op0=op0, op1=op1),
                  reads, writes)

        def cp(eng, out, in_, reads, writes):
            if eng == "act":
                S.add("act", lambda e: e.copy(out=out, in_=in_), reads, writes)
            else:
                S.add(eng, lambda e: e.tensor_copy(out=out, in_=in_), reads, writes)

        def recip(out, in_, reads, writes):
            S.add("dve", lambda e: e.reciprocal(out=out, in_=in_), reads, writes)

        def dma(q, out, in_, reads, writes, **kw):
            return S.add(q, lambda e: e.dma_start(out=out, in_=in_, **kw), reads, writes, dma=True)

        def rstd_from_ssq(ssq_ps, inv_n, eps, dst):
            t = rot(tf, "tf")
            act(t.ap, ssq_ps.ap, AF.Ln, [ssq_ps.b], [t.b], scale=inv_n, bias=eps_t[eps])
            act(dst.ap, t.ap, AF.Exp, [t.b], [dst.b], scale=-0.5)

        eps_tile = shA.alloc([128, 2], F32)
        eps_t = {NORM_EPS: eps_tile.ap[:, 0:1], SUBLN_EPS: eps_tile.ap[:, 1:2]}
        S.add("dve", lambda e: e.memset(eps_tile.ap[:, 0:1], NORM_EPS), [], [eps_tile.b])
        S.add("dve", lambda e: e.memset(eps_tile.ap[:, 1:2], SUBLN_EPS), [], [eps_tile.b])
        dma("sp", cst_f.ap, cst_d, [], [cst_f.b])
        cp("dve", cst_bf.ap, cst_f.ap, [cst_f.b], [cst_bf.b])
        gin = rot(tf, "tf")
        dma("sp", gin.ap[:, 0:128], norms_d, [], [gin.b])
        tr(psum[0].ap[:, 0:128], gin.ap[:, 0:128], ident_f, [gin.b, cst_f.b], [psum[0].b])
        cp("dve", gains.ap, psum[0].ap[:, 0:128], [psum[0].b], [gains.b])
        gin2 = rot(tf, "tf")
        dma("sp", gin2.ap[0:4, 0:128], subln_d, [], [gin2.b])
        tr(psum[1].ap[:, 0:4], gin2.ap[0:4, 0:128], ident_f[0:4, 0:4], [gin2.b, cst_f.b], [psum[1].b])
        for l in range(DEPTH):
            S.add("dve", lambda e, l=l: e.tensor_scalar(out=gsub.ap[:, l:l + 1], in0=psum[1].ap[:, l:l + 1],
                                                        scalar1=float(1.0 - lambda_init(l)), scalar2=None,
                                                        op0=ALU.mult), [psum[1].b], [gsub.b])
        lamt = [rot(tf, "tf") for _ in range(2)]
        lam_sb = lamt[0]
        dma("sp", lam_sb.ap, lam_d[0:1, 0:512].partition_broadcast(128), [], [lam_sb.b])
        dma("sp", lamt[1].ap, lam_d[0:1, 512:1024].partition_broadcast(128), [], [lamt[1].b])
        lprod = rot(tf, "tf")
        lsum = rot(tf, "tf")
        for l in range(DEPTH):
            src = lamt[l // 2].ap[:, (l % 2) * 256:(l % 2) * 256 + 256].rearrange("p (a b) -> p a b", a=4)
            for j in range(2):
                S.add("dve", lambda e, src=src, j=j, l=l: e.tensor_tensor(
                    out=lprod.ap[:, (2 * l + j) * 64:(2 * l + j + 1) * 64], in0=src[:, 2 * j, :],
                    in1=src[:, 2 * j + 1, :], op=ALU.mult), [lamt[l // 2].b], [lprod.b])
        S.add("dve", lambda e: e.tensor_reduce(out=lsum.ap[:, 0:8],
                                               in_=lprod.ap[:, 0:512].rearrange("p (a b) -> p a b", a=8),
                                               axis=AX.X, op=ALU.add), [lprod.b], [lsum.b])
        act(lsum.ap[:, 8:16], lsum.ap[:, 0:8], AF.Exp, [lsum.b], [lsum.b])
        for l in range(DEPTH):
            S.add("dve", lambda e, l=l: e.scalar_tensor_tensor(
                out=neglam.ap[:, l:l + 1], in0=lsum.ap[:, 8 + 2 * l + 1:8 + 2 * l + 2],
                scalar=float(-lambda_init(l)), in1=lsum.ap[:, 8 + 2 * l:8 + 2 * l + 1],
                op0=ALU.add, op1=ALU.subtract), [lsum.b], [neglam.b])

        def cast_w(dst, src, l, key, b):
            d2 = dst[l].rearrange("r (a b) -> (r a) b", b=b)
            s2 = src[l].rearrange("r (a b) -> (r a) b", b=b)
            dma("pool", d2, s2, [], [WB_b[key][l]])

        for l in range(n_layers):
            cast_w(wb_in, w_in_d, l, "in", 2048)
            cast_w(wb_a, w_a_d, l, "a", 1024)
            cast_w(wb_b, w_b_d, l, "b", 1024)
            cast_w(wb_out, w_out_d, l, "out", 1024)
            cast_w(wb_gu, w_gu_d, l, "gu", 1408)
            cast_w(wb_dn, w_dn_d, l, "dn", 1024)

        def layout_AC():
            ovA.reset()
            L = {}
            L["xbuf"] = [ovA.alloc([128, 8, TC], F32) for _ in range(2)]
            L["sqb"] = ovA.alloc([128, 8, TC], BF16)
            L["wbig"] = [ovA.alloc([128, 8, TC], BF16) for _ in range(4)]
            L["oac"] = ovA.alloc([128, 4, TC], BF16)
            L["obc"] = ovA.alloc([128, 4, TC], BF16)
            L["merged"] = ovA.alloc([128, 8, TC], BF16)
            L["M"] = ovA.alloc([128, 8, TC], F32)
            L["H2"] = ovA.alloc([128, 8, TC], BF16)
            L["ACT"] = ovA.alloc([128, NFC, TC], BF16)
            return L

        def layout_B():
            ovA.reset()
            L = {}
            L["cos"] = ovA.alloc([128, S_], F32)
            L["sin"] = ovA.alloc([128, S_], F32)
            L["QT"] = ovA.alloc([128, S_], BF16, nsub=NTC)
            L["KT"] = ovA.alloc([128, S_], BF16, nsub=NTC)
            L["VT"] = ovA.alloc([128, S_], BF16, nsub=NTC)
            L["VD"] = ovA.alloc([128, 32, 128], BF16, nsub=8)
            L["VB"] = ovA.alloc([128, 32, 2, 128], BF16, nsub=8)
            L["PT"] = [ovA.alloc([128, TC], BF16) for _ in range(4)]
            L["wsm"] = [ovA.alloc([128, 8, 128], BF16) for _ in range(6)]
            L["ACC"] = ovA.alloc([128, 2, S_], F32, nsub=2)
            L["DACC"] = [ovA.alloc([128, TC], F32) for _ in range(4)]
            return L

        LA = layout_AC()
        wi = {"n": 0}

        def x_to_xt():
            for tc in range(NTC):
                xin = LA["xbuf"][tc % 2]
                xin_ap = xin.ap.rearrange("p a b -> p (a b)").rearrange("p (a d) -> p a d", a=4)
                dma("sp", xin_ap, x_d[tc * TC:(tc + 1) * TC, :].rearrange("(a p) d -> p a d", p=128),
                    [], [xin.b])
                st = LA["M"]
                for c in range(8):
                    ps = psum[c % 4]
                    for a in range(4):
                        tr(ps.ap[:, a * 128:(a + 1) * 128], xin_ap[:, a, c * 128:(c + 1) * 128], ident_f,
                           [xin.b, cst_f.b], [ps.b])
                    cp("dve" if c % 2 == 0 else "act", st.ap[:, c, :], ps.ap, [ps.b], [st.b])
                dma("sp", XT_d[:, :, tc * TC:(tc + 1) * TC], st.ap, [st.b], [XT_b[tc]])

        x_to_xt()

        def phase_A(l):
            for tc in range(NTC):
                tsl = slice(tc * TC, (tc + 1) * TC)
                xc = LA["xbuf"][tc % 2]
                dma("sp", xc.ap, XT_d[:, :, tsl], [XT_b[tc]], [xc.b])
                sq = LA["sqb"]
                act(sq.ap, xc.ap, AF.Square, [xc.b], [sq.b])
                ps = psum[tc % 2]
                for c in range(8):
                    mm(ps.ap, ones_bf, sq.ap[:, c, :], c == 0, c == 7, [sq.b, cst_bf.b], [ps.b])
                rs = rot(rsr, "rs")
                rstd_from_ssq(ps, 1.0 / D_, NORM_EPS, rs)
                hb = rot(hbuf, "hb")
                for c in range(8):
                    stt(hb.ap[:, c, :], xc.ap[:, c, :], gain(0, l, c), rs.ap, ALU.mult, ALU.mult,
                        [xc.b, rs.b, gains.b], [hb.b])
                dma("sp", HT_d[:, :, tsl], hb.ap, [hb.b], [HT_b[tc]])

        def load_wsm(LB, l, col0):
            w = LB["wsm"][wi["n"] % 6]
            wi["n"] += 1
            dma("sp", w.ap, wb_in[l, :, col0:col0 + 128].rearrange("(k p) n -> p k n", p=128),
                [WB_b["in"][l]], [w.b])
            return w

        def project(LB, l, cols, dsts, ropes):
            ws = [load_wsm(LB, l, c0) for c0 in cols]
            pending = []

            def tail(ps, qb, dst, tc, k):
                tsl = slice(tc * TC, (tc + 1) * TC)
                ps2 = psum[3 + k % 2]
                mm(ps2.ap, perm_bf, qb.ap, True, True, [qb.b, cst_bf.b], [ps2.b])
                t1 = rot(tf, "tf")
                t2 = rot(tf, "tf")
                tt("dve", t1.ap, ps.ap, LB["cos"].ap[:, tsl], ALU.mult, [ps.b, LB["cos"].b, qb.b], [t1.b])
                tt("dve", t2.ap, ps2.ap, LB["sin"].ap[:, tsl], ALU.mult, [ps2.b, LB["sin"].b], [t2.b])
                tt("pool", dst.ap[:, tsl], t1.ap, t2.ap, ALU.add, [t1.b, t2.b], [dst.s[tc]])

            k = 0
            for tc in range(NTC):
                tsl = slice(tc * TC, (tc + 1) * TC)
                hc = rot(hbuf, "hb")
                dma("sp", hc.ap, HT_d[:, :, tsl], [HT_b[tc]], [hc.b])
                for j, (w, dst, rope) in enumerate(zip(ws, dsts, ropes)):
                    ps = psum[j % 3]
                    for kc in range(8):
                        mm(ps.ap, w.ap[:, kc, :], hc.ap[:, kc, :], kc == 0, kc == 7, [w.b, hc.b], [ps.b])
                    if rope:
                        qb = rot(tb, "tb")
                        cp("act", qb.ap, ps.ap, [ps.b], [qb.b])
                        new_p = (ps, qb, dst, tc, k)
                        k += 1
                    else:
                        cp("act", dst.ap[:, tsl], ps.ap, [ps.b], [dst.s[tc]])
                        new_p = None
                    while pending:
                        tail(*pending.pop(0))
                    if new_p is not None:
                        pending.append(new_p)
            while pending:
                tail(*pending.pop(0))

        def v_transposes(LB, d, dil_mode):
            VT = LB["VT"]
            nkc = 32 // d
            psb = [psum[5].ap.bitcast(BF16), psum[6].ap.bitcast(BF16)]
            for g4 in range(8):
                ps = psum[5 + g4 % 2]
                pv = psb[g4 % 2]
                for a in range(4):
                    ch = g4 * 4 + a
                    r, c = ch // nkc, ch % nkc
                    t0 = r + d * 128 * c
                    src = VT.ap[:, t0:t0 + 127 * d + 1:d] if d > 1 else VT.ap[:, t0:t0 + 128]
                    tr(pv[:, a * 128:(a + 1) * 128], src, ident_bf, VT.s + [cst_bf.b], [ps.b])
                if dil_mode:
                    VB = LB["VB"]
                    cp("dve", VB.ap[:, g4 * 4:(g4 + 1) * 4, :, 0:64],
                       pv[:, 0:512].rearrange("p (a h e) -> p a h e", a=4, h=2), [ps.b], [VB.s[g4]])
                else:
                    VD = LB["VD"]
                    cp("dve", VD.ap[:, g4 * 4:(g4 + 1) * 4, :],
                       pv[:, 0:512].rearrange("p (a e) -> p a e", a=4), [ps.b], [VD.s[g4]])

        def diff_head(LB, l, hd):
            QT, KT, VD = LB["QT"], LB["KT"], LB["VD"]
            project(LB, l, [128 * hd, 512 + 128 * hd, 1024 + 128 * hd], [QT, KT, LB["VT"]], [True, True, False])
            if upto == "B1":
                return
            v_transposes(LB, 1, False)
            if upto == "B2":
                return
            st_ring = [psum[0], psum[1], psum[2], psum[3]]
            num = [psum[4], psum[5]]
            den = [psum[6], psum[7]]
            ssq_ps = psum[7]
            PT = LB["PT"]
            DACC = LB["DACC"]
            ones_f = cst_f.ap[:, 512:640]
            pairs = [(qc, kc) for qc in range(NTC) for kc in range(32)]
            npairs = len(pairs)

            def QKp(p):
                qc, kc = pairs[p]
                for c in range(2):
                    st = st_ring[(2 * p + c) % 4]
                    mm(st.ap, KT.ap[64 * c:64 * c + 64, kc * 128:(kc + 1) * 128],
                       QT.ap[64 * c:64 * c + 64, qc * TC:(qc + 1) * TC], True, True, [KT.s[kc // 4], QT.s[qc]], [st.b])
                for c in range(2):
                    st = st_ring[(2 * p + c) % 4]
                    pt = PT[(2 * p + c) % 4]
                    act(pt.ap, st.ap, AF.Exp, [st.b], [pt.b], scale=0.125)

            def PVp(p):
                qc, kc = pairs[p]
                for c in range(2):
                    pt = PT[(2 * p + c) % 4]
                    mm(num[c].ap, VD.ap[:, kc, :], pt.ap, kc == 0, kc == 31, [VD.s[kc // 4], pt.b], [num[c].b])
                    acc = DACC[2 * (qc % 2) + c]
                    if kc == 0:
                        cp("dve", acc.ap, pt.ap, [pt.b], [acc.b])
                    else:
                        tt("dve", acc.ap, acc.ap, pt.ap, ALU.add, [acc.b, pt.b], [acc.b])

            keep = {}

            def fin_num(qc):
                nm = [rot(tf, "tf") for _ in range(2)]
                for c in range(2):
                    cp("dve", nm[c].ap, num[c].ap, [num[c].b], [nm[c].b])
                keep[qc] = [nm]

            def fin_a(qc):
                nm = keep[qc][0]
                dn = [rot(tf, "tf") for _ in range(2)]
                for c in range(2):
                    acc = DACC[2 * (qc % 2) + c]
                    mm(den[c].ap, ones_f, acc.ap, True, True, [acc.b, cst_f.b], [den[c].b])
                for c in range(2):
                    recip(dn[c].ap, den[c].ap, [den[c].b], [dn[c].b])
                for c in range(2):
                    tt("dve", nm[c].ap, nm[c].ap, dn[c].ap, ALU.mult, [nm[c].b, dn[c].b], [nm[c].b])
                a_ = rot(tf, "tf")
                stt(a_.ap, nm[1].ap, neglam.ap[:, l:l + 1], nm[0].ap, ALU.mult, ALU.add,
                    [nm[0].b, nm[1].b, neglam.b], [a_.b])
                sq = rot(tb, "tb")
                tt("pool", sq.ap, a_.ap, a_.ap, ALU.mult, [a_.b], [sq.b])
                keep[qc] = (a_, sq)

            def fin_b(qc):
                a_, sq = keep.pop(qc)
                mm(ssq_ps.ap, ones_bf, sq.ap, True, True, [sq.b, cst_bf.b], [ssq_ps.b])
                rs = rot(rsr, "rs")
                rstd_from_ssq(ssq_ps, 1.0 / 128, SUBLN_EPS, rs)
                ob = rot(obuf, "ob")
                stt(ob.ap, a_.ap, gsub.ap[:, l:l + 1], rs.ap, ALU.mult, ALU.mult, [a_.b, rs.b, gsub.b], [ob.b])
                dma("sp", OA_d[:, hd, qc * TC:(qc + 1) * TC], ob.ap, [ob.b], [OA_b[hd][qc]])

            pend = []
            QKp(0)
            for p in range(npairs):
                if p + 1 < npairs:
                    QKp(p + 1)
                PVp(p)
                qc, kc = pairs[p]
                if kc == 31:
                    fin_num(qc)
                    pend.append((p + 2, fin_a, qc))
                    pend.append((p + 12, fin_b, qc))
                    pend.sort(key=lambda t: t[0])
                while pend and pend[0][0] <= p:
                    _, f, q_ = pend.pop(0)
                    f(q_)
            while pend:
                _, f, q_ = pend.pop(0)
                f(q_)

        def dil_pair(LB, l, j):
            QT, KT, VB, ACC = LB["QT"], LB["KT"], LB["VB"], LB["ACC"]
            wstate = {"n": 0}
            for g, d in enumerate(DIL):
                base = 1536 + 1536 * g + 128 * j
                project(LB, l, [base, base + 512, base + 1024], [QT, KT, LB["VT"]], [True, True, False])
                v_transposes(LB, d, True)
                Lg = S_ // d
                nkc = Lg // 128
                items = [(h, r, c) for h in range(2) for r in range(d) for c in range(nkc)]
                n = len(items)
                LA = 2
                PT = LB["PT"]
                st_ring = [psum[0], psum[1], psum[2]]
                wps = [None, None]

                def toks(r, p0, p1, d=d):
                    return slice(r + d * p0, r + d * (p1 - 1) + 1, d) if d > 1 else slice(p0, p1)

                def geom(c, Lg=Lg):
                    q0 = max(0, 128 * c - 64)
                    q1 = min(Lg, 128 * c + 192)
                    return q0, q1

                def QK(i):
                    h, r, c = items[i]
                    rows = slice(64 * h, 64 * h + 64)
                    q0, q1 = geom(c)
                    nq = q1 - q0
                    moff = q0 - (128 * c - 64)
                    st = st_ring[i % 3]
                    mm(st.ap[:, 0:nq], KT.ap[rows, toks(r, 128 * c, 128 * c + 128)], QT.ap[rows, toks(r, q0, q1)],
                       True, False, KT.s + QT.s, [st.b])
                    mm(st.ap[:, 0:nq], ident_bf, mask_bf[:, moff:moff + nq], False, True, [cst_bf.b], [st.b])
                    pt = PT[i % 4]
                    act(pt.ap[:, 0:nq], st.ap[:, 0:nq], AF.Exp, [st.b], [pt.b], scale=0.125)

                def PV(i):
                    h, r, c = items[i]
                    q0, q1 = geom(c)
                    nq = q1 - q0
                    pt = PT[i % 4]
                    ch = r * nkc + c
                    va = VB.ap[:, ch, h, :]
                    split = 128 * c + 64 - q0
                    for w, (c0, c1) in ((c, (0, split)), (c + 1, (split, nq))):
                        first = (w == c + 1) or (c == 0)
                        last = (w == c) or (c == nkc - 1)
                        if first:
                            wps[w % 2] = psum[3 + wstate["n"] % 4]
                            wstate["n"] += 1
                        wp = wps[w % 2]
                        mm(wp.ap[:, 0:c1 - c0], va, pt.ap[:, c0:c1], first, last, [VB.s[ch // 4], pt.b], [wp.b])
                        if last:
                            p0 = max(0, 128 * w - 64)
                            p1 = min(Lg, 128 * w + 64)
                            assert p1 - p0 == c1 - c0, (p0, p1, c0, c1)
                            dst = ACC.ap[:, h, toks(r, p0, p1)]
                            if g == 0:
                                cp("dve", dst, wp.ap[:, 0:p1 - p0], [wp.b], [ACC.s[h]])
                            else:
                                tt("dve", dst, wp.ap[:, 0:p1 - p0], dst, ALU.add, [wp.b, ACC.s[h]], [ACC.s[h]])

                for i in range(min(LA, n)):
                    QK(i)
                for i in range(n):
                    if i + LA < n:
                        QK(i + LA)
                    PV(i)
            for tc in range(NTC):
                tsl = slice(tc * TC, (tc + 1) * TC)
                ob = rot(obuf, "ob")
                for h in range(2):
                    rd = rot(tf, "tf")
                    recip(rd.ap[0:64, :], ACC.ap[64:128, h, tsl], [ACC.s[h]], [rd.b])
                    tt("dve", ob.ap[64 * h:64 * h + 64, :], ACC.ap[0:64, h, tsl], rd.ap[0:64, :], ALU.mult,
                       [ACC.s[h], rd.b], [ob.b])
                dma("sp", OB_d[:, j, tsl], ob.ap, [ob.b], [OB_b[j][tc]])

        def phase_B(l, LB):
            for hd in range(nheads[0]):
                mark("diff%d.%d" % (l, hd))
                diff_head(LB, l, hd)
            for j in range(nheads[1]):
                mark("dil%d.%d" % (l, j))
                dil_pair(LB, l, j)

        def load_wbig(src_ap, kcs, buf):
            w = LA["wbig"][wi["n"] % 4]
            wi["n"] += 1
            dma("sp", w.ap[:, 0:kcs, :], src_ap.rearrange("(k p) n -> p k n", p=128), [buf], [w.b])
            return w

        def post_norm_residual(l, ty, xc):
            M = LA["M"]
            sq = LA["sqb"]
            act(sq.ap, M.ap, AF.Square, [M.b], [sq.b])
            ps = psum[7]
            for c in range(8):
                mm(ps.ap, ones_bf, sq.ap[:, c, :], c == 0, c == 7, [sq.b, cst_bf.b], [ps.b])
            rs = rot(rsr, "rs")
            rstd_from_ssq(ps, 1.0 / D_, NORM_EPS, rs)
            for c in range(8):
                t = rot(tf, "tf")
                stt(t.ap, M.ap[:, c, :], gain(ty, l, c), rs.ap, ALU.mult, ALU.mult, [M.b, rs.b, gains.b], [t.b])
                tt("pool", xc.ap[:, c, :], xc.ap[:, c, :], t.ap, ALU.add, [xc.b, t.b], [xc.b])

        def phase_C(l):
            pr = {"n": 0}

            def nps():
                p = psum[pr["n"] % 7]
                pr["n"] += 1
                return p

            for tc in range(NTC):
                tsl = slice(tc * TC, (tc + 1) * TC)
                xc = LA["xbuf"][tc % 2]
                dma("sp", xc.ap, XT_d[:, :, tsl], [XT_b[tc]], [xc.b])
                hc = rot(hbuf, "hb")
                dma("sp", hc.ap, HT_d[:, :, tsl], [HT_b[tc]], [hc.b])
                oac, obc = LA["oac"], LA["obc"]
                dma("sp", oac.ap, OA_d[:, :, tsl], [OA_b[h][tc] for h in range(4)], [oac.b])
                dma("sp", obc.ap, OB_d[:, :, tsl], [OB_b[h][tc] for h in range(4)], [obc.b])
                merged = LA["merged"]
                for half in range(2):
                    wgA = load_wbig(wb_in[l, :, 6144 + 512 * half:6144 + 512 * half + 512], 8, WB_b["in"][l])
                    wgB = load_wbig(wb_in[l, :, 7168 + 512 * half:7168 + 512 * half + 512], 8, WB_b["in"][l])
                    wa = load_wbig(wb_a[l, :, 512 * half:512 * half + 512], 4, WB_b["a"][l])
                    wb_ = load_wbig(wb_b[l, :, 512 * half:512 * half + 512], 4, WB_b["b"][l])
                    for jj in range(4):
                        j = 4 * half + jj
                        js = slice(128 * jj, 128 * jj + 128)
                        parts = []
                        for (wg, wp_, oc) in ((wgA, wa, oac), (wgB, wb_, obc)):
                            pg = nps()
                            for kc in range(8):
                                mm(pg.ap, wg.ap[:, kc, js], hc.ap[:, kc, :], kc == 0, kc == 7, [wg.b, hc.b], [pg.b])
                            sg = rot(tf, "tf")
                            act(sg.ap, pg.ap, AF.Sigmoid, [pg.b], [sg.b])
                            pp = nps()
                            for kc in range(4):
                                mm(pp.ap, wp_.ap[:, kc, js], oc.ap[:, kc, :], kc == 0, kc == 3, [wp_.b, oc.b], [pp.b])
                            t = rot(tf, "tf")
                            tt("dve", t.ap, pp.ap, sg.ap, ALU.mult, [pp.b, sg.b], [t.b])
                            parts.append(t)
                        tt("pool", merged.ap[:, j, :], parts[0].ap, parts[1].ap, ALU.add,
                           [parts[0].b, parts[1].b], [merged.b])
                if upto == "C0":
                    dma("sp", HT_d[:, :, tsl], merged.ap, [merged.b], [HT_b[tc]])
                    continue
                M = LA["M"]
                for half in range(2):
                    wo = load_wbig(wb_out[l, :, 512 * half:512 * half + 512], 8, WB_b["out"][l])
                    for jj in range(4):
                        j = 4 * half + jj
                        pm = nps()
                        for kc in range(8):
                            mm(pm.ap, wo.ap[:, kc, 128 * jj:128 * jj + 128], merged.ap[:, kc, :], kc == 0, kc == 7,
                               [wo.b, merged.b], [pm.b])
                        cp("act" if j % 2 else "dve", M.ap[:, j, :], pm.ap, [pm.b], [M.b])
                if upto == "C0b":
                    dma("sp", XT_d[:, :, tsl], M.ap, [M.b], [XT_b[tc]])
                    continue
                post_norm_residual(l, 1, xc)
                if upto == "C1":
                    dma("sp", XT_d[:, :, tsl], xc.ap, [xc.b], [XT_b[tc]])
                    continue
                sq = LA["sqb"]
                act(sq.ap, xc.ap, AF.Square, [xc.b], [sq.b])
                ps = psum[7]
                for c in range(8):
                    mm(ps.ap, ones_bf, sq.ap[:, c, :], c == 0, c == 7, [sq.b, cst_bf.b], [ps.b])
                rs = rot(rsr, "rs")
                rstd_from_ssq(ps, 1.0 / D_, NORM_EPS, rs)
                H2 = LA["H2"]
                for c in range(8):
                    stt(H2.ap[:, c, :], xc.ap[:, c, :], gain(2, l, c), rs.ap, ALU.mult, ALU.mult,
                        [xc.b, rs.b, gains.b], [H2.b])
                ACTT = LA["ACT"]
                for i4 in range(0, NFC, 4):
                    n4 = min(4, NFC - i4)
                    wg = load_wbig(wb_gu[l, :, 128 * i4:128 * (i4 + n4)], 8, WB_b["gu"][l]) if n4 == 4 else None
                    if wg is None:
                        wg = LA["wbig"][wi["n"] % 4]
                        wi["n"] += 1
                        dma("sp", wg.ap[:, :, 0:128 * n4],
                            wb_gu[l, :, 128 * i4:128 * (i4 + n4)].rearrange("(k p) n -> p k n", p=128),
                            [WB_b["gu"][l]], [wg.b])
                    wu = LA["wbig"][wi["n"] % 4]
                    wi["n"] += 1
                    dma("sp", wu.ap[:, :, 0:128 * n4],
                        wb_gu[l, :, FF + 128 * i4:FF + 128 * (i4 + n4)].rearrange("(k p) n -> p k n", p=128),
                        [WB_b["gu"][l]], [wu.b])
                    for ii in range(n4):
                        i = i4 + ii
                        isl = slice(128 * ii, 128 * ii + 128)
                        pg = nps()
                        for kc in range(8):
                            mm(pg.ap, wg.ap[:, kc, isl], H2.ap[:, kc, :], kc == 0, kc == 7, [wg.b, H2.b], [pg.b])
                        pu = nps()
                        for kc in range(8):
                            mm(pu.ap, wu.ap[:, kc, isl], H2.ap[:, kc, :], kc == 0, kc == 7, [wu.b, H2.b], [pu.b])
                        sg = rot(tf, "tf")
                        act(sg.ap, pg.ap, AF.Silu, [pg.b], [sg.b])
                        tt("dve", ACTT.ap[:, i, :], pu.ap, sg.ap, ALU.mult, [pu.b, sg.b], [ACTT.b])
                for half in range(2):
                    pms = [nps() for _ in range(4)]
                    for (k0, kn) in ((0, 8), (8, 8), (16, 6)):
                        wd = load_wbig(wb_dn[l, 128 * k0:128 * (k0 + kn), 512 * half:512 * half + 512], kn,
                                       WB_b["dn"][l])
                        for jj in range(4):
                            for kk in range(kn):
                                kc = k0 + kk
                                mm(pms[jj].ap, wd.ap[:, kk, 128 * jj:128 * jj + 128], ACTT.ap[:, kc, :],
                                   kc == 0, kc == NFC - 1, [wd.b, ACTT.b], [pms[jj].b])
                    for jj in range(4):
                        j = 4 * half + jj
                        cp("act" if j % 2 else "dve", M.ap[:, j, :], pms[jj].ap, [pms[jj].b], [M.b])
                post_norm_residual(l, 3, xc)
                dma("sp", XT_d[:, :, tsl], xc.ap, [xc.b], [XT_b[tc]])

        def xt_to_out():
            for tc in range(NTC):
                tsl = slice(tc * TC, (tc + 1) * TC)
                xc = LA["xbuf"][tc % 2]
                dma("sp", xc.ap, XT_d[:, :, tsl], [XT_b[tc]], [xc.b])
                st = LA["M"]
                st_ap = st.ap.rearrange("p a b -> p (a b)").rearrange("p (a d) -> p a d", a=4)
                for a in range(4):
                    for half in range(2):
                        ps = psum[(2 * a + half) % 4]
                        for cc in range(4):
                            c = 4 * half + cc
                            tr(ps.ap[:, cc * 128:(cc + 1) * 128], xc.ap[:, c, a * 128:(a + 1) * 128], ident_f,
                               [xc.b, cst_f.b], [ps.b])
                        cp("dve" if half else "act", st_ap[:, a, 512 * half:512 * half + 512], ps.ap, [ps.b], [st.b])
                dma("sp", out_d[tc * TC:(tc + 1) * TC, :].rearrange("(a p) d -> p a d", p=128), st_ap,
                    [st.b], [out_bs[tc]])

        S.marks = []
        mark = lambda nm: S.marks.append((nm, len(S.prog["pe"])))
        for l in range(n_layers):
            if upto == "prologue":
                break
            mark("A%d" % l)
            phase_A(l)
            if upto == "A":
                break
            S.barrier()
            LB = layout_B()
            dma("sp", LB["cos"].ap, cos_d, [], [LB["cos"].b])
            dma("sp", LB["sin"].ap, sin_d, [], [LB["sin"].b])
            for ch in range(32):
                for h in range(2):
                    S.add("pool", lambda e, ch=ch, h=h, vb=LB["VB"]: e.memset(vb.ap[:, ch, h, 64:128], 1.0),
                          [], [LB["VB"].s[ch // 4]])
            if upto == "B0":
                break
            phase_B(l, LB)
            if upto in ("B1", "B2"):
                break
            S.barrier()
            LA = layout_AC()
            if upto == "B":
                break
            mark("C%d" % l)
            phase_C(l)
        mark("epi")
        xt_to_out()
        mark("end")
        build.marks = S.marks
        if dbg:
            for name, shape, dt in dbg:
                src = {"XT": XT_d, "HT": HT_d, "OAT": OA_d, "OBT": OB_d}[name]
                dma("sp", dbg_d[name], src, sum([XT_b, HT_b] + OA_b + OB_b, []), [out_bs[NTC + len(dbg_d) % 4]])
        S.add("sp", None, out_bs, [])
        S.emit(nc, block, sems, rings)
    return nc


def _rope_tables():
    inv = (1.0 / (np.float32(10000.0) ** (np.arange(0, 64, 2, dtype=np.float32) / np.float32(64)))).astype(np.float32)
    ang = (np.arange(S_, dtype=np.float32)[:, None] * inv[None, :]).astype(np.float32)
    cos = np.cos(ang).astype(np.float32).T
    sin = np.sin(ang).astype(np.float32).T
    cos2 = np.concatenate([cos, cos, cos, cos], axis=0)
    ssin = np.concatenate([-sin, sin, -sin, sin], axis=0)
    return np.ascontiguousarray(cos2), np.ascontiguousarray(ssin)


def _consts():
    c = np.zeros((128, 640), np.float32)
    c[:, 0:128] = np.eye(128, dtype=np.float32)
    for m in range(128):
        k = m + 32 if (m % 64) < 32 else m - 32
        c[k, 128 + m] = 1.0
    i = np.arange(128)[:, None]
    jj = np.arange(256)[None, :]
    c[:, 256:512] = np.where((jj >= i) & (jj <= i + 128), 0.0, MASK_NEG)
    c[:, 512:640] = 1.0
    return c


_NC_CACHE = {}


def make_inputs(x, w_in, w_a, w_b, w_out, lam, subln_g, norm_mix_pre, norm_mix_post,
                w_gate_up, w_down, norm_ffn_pre, norm_ffn_post):
    f = lambda a: np.ascontiguousarray(np.asarray(a, dtype=np.float32))
    cos2, ssin = _rope_tables()
    norms = np.stack([f(norm_mix_pre), f(norm_mix_post), f(norm_ffn_pre), f(norm_ffn_post)], 0)
    norms = np.ascontiguousarray(norms.reshape(4 * DEPTH * 8, 128))
    shared = {
        "w_in": f(w_in), "w_a": f(w_a), "w_b": f(w_b), "w_out": f(w_out),
        "lam": f(lam).reshape(1, -1), "subln_g": f(subln_g), "norms": norms,
        "w_gate_up": f(w_gate_up), "w_down": f(w_down),
        "cos2": cos2, "ssin": ssin, "consts": _consts(),
    }
    x = f(x)
    return [dict(shared, x=x[i]) for i in range(N_CORES)]


def kernel(**inputs):
    in_maps = make_inputs(**inputs)
    if "nc" not in _NC_CACHE:
        _NC_CACHE["nc"] = build()
    res = run_bass_kernel_spmd(_NC_CACHE["nc"], in_maps, core_ids=list(range(N_CORES)))
    return np.stack([np.asarray(r["out"], dtype=np.float32) for r in res.results], axis=0)
```

```python
import contextlib
import math
import numpy as np
import concourse.bass as bass
import concourse.mybir as mybir
from concourse.bass_utils import run_bass_kernel_spmd

F32 = mybir.dt.float32
BF16 = mybir.dt.bfloat16
U8 = mybir.dt.uint8
ALU = mybir.AluOpType
AF = mybir.ActivationFunctionType
AX = mybir.AxisListType

S_ = 4096
D_ = 1024
DEPTH = 4
TC = 512
NTC = S_ // TC
FF = 2816
NFC = FF // 128
IN_COLS = 8192
DIL = (1, 4, 16)
NORM_EPS = 1e-6
SUBLN_EPS = 1e-5
MASK_NEG = -30000.0
N_CORES = 8
RING = 8
RINGS = {"sp": 8, "pool": 2}


class Buf:
    __slots__ = ("w", "r", "excl")

    def __init__(self, excl=False):
        self.w = None
        self.r = {}
        self.excl = excl


class Op:
    __slots__ = ("eng", "fn", "deps", "seq", "sig", "sigval", "dma", "dman", "waits", "nobar")


class Sched:
    ENG = ("pe", "act", "dve", "pool", "sp")

    def __init__(self):
        self.prog = {e: [] for e in self.ENG}
        self.ndma = {e: 0 for e in self.ENG}
        self.dma_since_bar = []

    def add(self, eng, fn, reads=(), writes=(), dma=False, extra=()):
        op = Op()
        op.eng = eng
        op.fn = fn
        op.dma = dma
        op.sig = False
        op.sigval = 0
        op.dman = -1
        op.nobar = False
        deps = {}
        for b in reads:
            if b.w is not None:
                deps[id(b.w)] = b.w
            if b.excl:
                for r in b.r.values():
                    if r.eng != eng:
                        deps[id(r)] = r
        for b in writes:
            if b.w is not None:
                deps[id(b.w)] = b.w
            for r in b.r.values():
                deps[id(r)] = r
        for d in extra:
            deps[id(d)] = d
        op.deps = list(deps.values())
        op.seq = len(self.prog[eng])
        self.prog[eng].append(op)
        if dma:
            op.dman = self.ndma[eng]
            self.ndma[eng] += 1
            self.dma_since_bar.append(op)
        for b in reads:
            b.r[(eng, op.seq) if dma else eng] = op
        for b in writes:
            b.w = op
            b.r = {}
        return op

    def barrier(self):
        last = []
        for e in self.ENG:
            for op in reversed(self.prog[e]):
                if op.fn is not None and not op.dma:
                    last.append(op)
                    break
        bar = self.add("sp", self.bar_fn, extra=last + [d for d in self.dma_since_bar if not d.nobar], dma=True)
        self.dma_since_bar = []
        for e in self.ENG:
            if e != "sp":
                self.add(e, None, extra=[bar])
        return bar

    def emit(self, nc, block, sems, rings):
        for eng in self.ENG:
            waited = {}
            waited_dma = set()
            for op in self.prog[eng]:
                op.waits = []
                best = {}
                for d in op.deps:
                    if d.dma:
                        if id(d) not in waited_dma:
                            waited_dma.add(id(d))
                            op.waits.append(d)
                    else:
                        if d.eng == "pe" and eng == "pe":
                            continue
                        if d.eng not in best or best[d.eng].seq < d.seq:
                            best[d.eng] = d
                for pe_, d in best.items():
                    if waited.get(pe_, -1) >= d.seq:
                        continue
                    waited[pe_] = d.seq
                    d.sig = True
                    op.waits.append(d)
        for eng in self.ENG:
            cnt = 0
            for op in self.prog[eng]:
                if not op.dma and op.sig:
                    cnt += 1
                    op.sigval = cnt
        hook = {"pe": block.tensor, "act": block.scalar, "dve": block.vector,
                "pool": block.gpsimd, "sp": block.sync}

        def semval(d):
            if d.dma:
                return rings[d.eng][d.dman % RINGS[d.eng]], 16 * (d.dman // RINGS[d.eng] + 1)
            return sems[d.eng], d.sigval

        for eng in self.ENG:
            ops = self.prog[eng]

            def body(e, ops=ops, eng=eng):
                for op in ops:
                    for d in op.waits:
                        s, v = semval(d)
                        e.wait_ge(s, v)
                    if op.dma:
                        n = op.dman
                        R_ = RINGS[eng]
                        if n >= R_:
                            e.wait_ge(rings[eng][n % R_], 16 * (n // R_))
                        ins = op.fn(e)
                        ins.then_inc(rings[eng][n % R_], 16)
                    elif op.fn is not None:
                        ins = op.fn(e)
                        if op.sig:
                            ins.then_inc(sems[eng], 1)
                    else:
                        assert not op.sig

            hook[eng](body)


class T:
    def __init__(self, ap, nsub=0):
        self.ap = ap
        self.b = Buf()
        self.s = [Buf() for _ in range(nsub)]


class Arena:
    def __init__(self, base_ap, nbytes):
        self.base = base_ap
        self.nbytes = nbytes
        self.off = 0

    def reset(self):
        self.off = 0

    def alloc(self, shape, dtype, nsub=0):
        esz = 4 if dtype == F32 else 2
        n = int(np.prod(shape[1:]))
        nb = n * esz
        nb_al = (nb + 63) // 64 * 64
        assert self.off + nb_al <= self.nbytes, (self.off, nb_al, self.nbytes)
        ap = self.base[:, self.off:self.off + nb].bitcast(dtype)
        self.off += nb_al
        if len(shape) == 3:
            ap = ap.rearrange("p (a b) -> p a b", a=shape[1])
        elif len(shape) == 4:
            ap = ap.rearrange("p (a b c) -> p a b c", a=shape[1], b=shape[2])
        return T(ap, nsub)


def lambda_init(l):
    return 0.8 - 0.6 * math.exp(-0.3 * l)


def build(n_layers=DEPTH, dbg=None, upto=None, nheads=(4, 4), WD=DEPTH):
    nc = bass.Bass("TRN2", target_bir_lowering=False)
    S = Sched()

    def dram(name, shape, dt, kind):
        return nc.dram_tensor(name, list(shape), dt, kind=kind).ap()

    x_d = dram("x", [S_, D_], F32, "ExternalInput")
    w_in_d = dram("w_in", [WD, D_, IN_COLS], F32, "ExternalInput")
    w_a_d = dram("w_a", [WD, 512, D_], F32, "ExternalInput")
    w_b_d = dram("w_b", [WD, 512, D_], F32, "ExternalInput")
    w_out_d = dram("w_out", [WD, D_, D_], F32, "ExternalInput")
    lam_d = dram("lam", [1, DEPTH * 4 * 64], F32, "ExternalInput")
    subln_d = dram("subln_g", [DEPTH, 128], F32, "ExternalInput")
    norms_d = dram("norms", [4 * DEPTH * 8, 128], F32, "ExternalInput")
    w_gu_d = dram("w_gate_up", [WD, D_, 2 * FF], F32, "ExternalInput")
    w_dn_d = dram("w_down", [WD, FF, D_], F32, "ExternalInput")
    cos_d = dram("cos2", [128, S_], F32, "ExternalInput")
    sin_d = dram("ssin", [128, S_], F32, "ExternalInput")
    cst_d = dram("consts", [128, 640], F32, "ExternalInput")
    out_d = dram("out", [S_, D_], F32, "ExternalOutput")

    XT_d = dram("XT", [128, 8, S_], F32, "Internal")
    HT_d = dram("HT", [128, 8, S_], BF16, "Internal")
    OA_d = dram("OAT", [128, 4, S_], BF16, "Internal")
    OB_d = dram("OBT", [128, 4, S_], BF16, "Internal")
    wb_in = dram("wb_in", [DEPTH, D_, IN_COLS], BF16, "Internal")
    wb_a = dram("wb_a", [DEPTH, 512, D_], BF16, "Internal")
    wb_b = dram("wb_b", [DEPTH, 512, D_], BF16, "Internal")
    wb_out = dram("wb_out", [DEPTH, D_, D_], BF16, "Internal")
    wb_gu = dram("wb_gu", [DEPTH, D_, 2 * FF], BF16, "Internal")
    wb_dn = dram("wb_dn", [DEPTH, FF, D_], BF16, "Internal")
    bar_d = dram("bar_scratch", [2, 64], F32, "Internal")
    S.bar_fn = lambda e: e.dma_start(out=bar_d[1:2, :], in_=cst_d[0:1, 0:64])
    dbg_d = {}
    if dbg:
        for name, shape, dt in dbg:
            dbg_d[name] = dram("dbg_" + name, shape, dt, "ExternalOutput")

    XT_b = [Buf() for _ in range(NTC)]
    HT_b = [Buf() for _ in range(NTC)]
    OA_b = [[Buf() for _ in range(NTC)] for _ in range(4)]
    OB_b = [[Buf() for _ in range(NTC)] for _ in range(4)]
    WB_b = {k: [Buf() for _ in range(DEPTH)] for k in ("in", "a", "b", "out", "gu", "dn")}
    out_bs = [Buf() for _ in range(NTC + 4)]

    es = contextlib.ExitStack()
    with es:
        sems = {e: es.enter_context(nc.semaphore("sem_" + e)) for e in Sched.ENG}
        rings = {e: [es.enter_context(nc.semaphore("ring_%s%d" % (e, i))) for i in range(RING)]
                 for e in ("sp", "pool")}
        SHARED = 50 * 1024
        OVL = 140 * 1024
        sh_t = es.enter_context(nc.sbuf_tensor("shared", [128, SHARED], U8))
        ov_t = es.enter_context(nc.sbuf_tensor("overlay", [128, OVL], U8))
        shA = Arena(sh_t[:], SHARED)
        ovA = Arena(ov_t[:], OVL)
        psum = []
        for i in range(8):
            pt_ = es.enter_context(nc.psum_tensor("ps%d" % i, [128, 512], F32))
            psum.append(T(pt_[:]))
            psum[-1].b.excl = True
        block = es.enter_context(nc.Block())

        cst_f = shA.alloc([128, 640], F32)
        ident_f = cst_f.ap[:, 0:128]
        cst_bf = shA.alloc([128, 640], BF16)
        ident_bf = cst_bf.ap[:, 0:128]
        perm_bf = cst_bf.ap[:, 128:256]
        mask_bf = cst_bf.ap[:, 256:512]
        ones_bf = cst_bf.ap[:, 512:640]
        gains = shA.alloc([128, 128], F32)
        gsub = shA.alloc([128, 4], F32)
        neglam = shA.alloc([128, 4], F32)
        hbuf = [shA.alloc([128, 8, TC], BF16) for _ in range(2)]
        tf = [shA.alloc([128, TC], F32) for _ in range(8)]
        tb = [shA.alloc([128, TC], BF16) for _ in range(4)]
        obuf = [shA.alloc([128, TC], BF16) for _ in range(2)]
        rsr = [shA.alloc([128, TC], F32) for _ in range(2)]
        rr = {"tf": 0, "tb": 0, "ob": 0, "hb": 0, "rs": 0}

        def rot(lst, key):
            t = lst[rr[key] % len(lst)]
            rr[key] += 1
            return t

        def gain(ty, l, c):
            i = (ty * DEPTH + l) * 8 + c
            return gains.ap[:, i:i + 1]

        def mm(out, lhsT, rhs, start, stop, reads, writes):
            S.add("pe", lambda e: e.matmul(out, lhsT=lhsT, rhs=rhs, start=start, stop=stop), reads, writes)

        def tr(out, in_, ident, reads, writes):
            S.add("pe", lambda e: e.transpose(out=out, in_=in_, identity=ident), reads, writes)

        def act(out, in_, func, reads, writes, scale=None, bias=None):
            kw = {}
            if scale is not None:
                kw["scale"] = scale
            if bias is not None:
                kw["bias"] = bias
            S.add("act", lambda e: e.activation(out=out, in_=in_, func=func, **kw), reads, writes)

        def tt(eng, out, in0, in1, op, reads, writes):
            S.add(eng, lambda e: e.tensor_tensor(out=out, in0=in0, in1=in1, op=op), reads, writes)

        def stt(out, in0, scalar, in1, op0, op1, reads, writes):
            S.add("dve", lambda e: e.scalar_tensor_tensor(out=out, in0=in0, scalar=scalar, in1=in1, op0=op0, op1=op1),
                  reads, writes)

        def cp(eng, out, in_, reads, writes):
            if eng == "act":
                S.add("act", lambda e: e.copy(out=out, in_=in_), reads, writes)
            else:
                S.add(eng, lambda e: e.tensor_copy(out=out, in_=in_), reads, writes)

        def recip(out, in_, reads, writes):
            S.add("dve", lambda e: e.reciprocal(out=out, in_=in_), reads, writes)

        def dma(q, out, in_, reads, writes, **kw):
            return S.add(q, lambda e: e.dma_start(out=out, in_=in_, **kw), reads, writes, dma=True)

        def rstd_from_ssq(ssq_ps, inv_n, eps, dst):
            t = rot(tf, "tf")
            act(t.ap, ssq_ps.ap, AF.Ln, [ssq_ps.b], [t.b], scale=inv_n, bias=eps_t[eps])
            act(dst.ap, t.ap, AF.Exp, [t.b], [dst.b], scale=-0.5)

        eps_tile = shA.alloc([128, 2], F32)
        eps_t = {NORM_EPS: eps_tile.ap[:, 0:1], SUBLN_EPS: eps_tile.ap[:, 1:2]}
        S.add("dve", lambda e: e.memset(eps_tile.ap[:, 0:1], NORM_EPS), [], [eps_tile.b])
        S.add("dve", lambda e: e.memset(eps_tile.ap[:, 1:2], SUBLN_EPS), [], [eps_tile.b])
        dma("sp", cst_f.ap, cst_d, [], [cst_f.b])
        cp("dve", cst_bf.ap, cst_f.ap, [cst_f.b], [cst_bf.b])
        gin = rot(tf, "tf")
        dma("sp", gin.ap[:, 0:128], norms_d, [], [gin.b])
        tr(psum[0].ap[:, 0:128], gin.ap[:, 0:128], ident_f, [gin.b, cst_f.b], [psum[0].b])
        cp("dve", gains.ap, psum[0].ap[:, 0:128], [psum[0].b], [gains.b])
        gin2 = rot(tf, "tf")
        dma("sp", gin2.ap[0:4, 0:128], subln_d, [], [gin2.b])
        tr(psum[1].ap[:, 0:4], gin2.ap[0:4, 0:128], ident_f[0:4, 0:4], [gin2.b, cst_f.b], [psum[1].b])
        for l in range(DEPTH):
            S.add("dve", lambda e, l=l: e.tensor_scalar(out=gsub.ap[:, l:l + 1], in0=psum[1].ap[:, l:l + 1],
                                                        scalar1=float(1.0 - lambda_init(l)), scalar2=None,
                                                        op0=ALU.mult), [psum[1].b], [gsub.b])
        lamt = [rot(tf, "tf") for _ in range(2)]
        lam_sb = lamt[0]
        dma("sp", lam_sb.ap, lam_d[0:1, 0:512].partition_broadcast(128), [], [lam_sb.b])
        dma("sp", lamt[1].ap, lam_d[0:1, 512:1024].partition_broadcast(128), [], [lamt[1].b])
        lprod = rot(tf, "tf")
        lsum = rot(tf, "tf")
        for l in range(DEPTH):
            src = lamt[l // 2].ap[:, (l % 2) * 256:(l % 2) * 256 + 256].rearrange("p (a b) -> p a b", a=4)
            for j in range(2):
                S.add("dve", lambda e, src=src, j=j, l=l: e.tensor_tensor(
                    out=lprod.ap[:, (2 * l + j) * 64:(2 * l + j + 1) * 64], in0=src[:, 2 * j, :],
                    in1=src[:, 2 * j + 1, :], op=ALU.mult), [lamt[l // 2].b], [lprod.b])
        S.add("dve", lambda e: e.tensor_reduce(out=lsum.ap[:, 0:8],
                                               in_=lprod.ap[:, 0:512].rearrange("p (a b) -> p a b", a=8),
                                               axis=AX.X, op=ALU.add), [lprod.b], [lsum.b])
        act(lsum.ap[:, 8:16], lsum.ap[:, 0:8], AF.Exp, [lsum.b], [lsum.b])
        for l in range(DEPTH):
            S.add("dve", lambda e, l=l: e.scalar_tensor_tensor(
                out=neglam.ap[:, l:l + 1], in0=lsum.ap[:, 8 + 2 * l + 1:8 + 2 * l + 2],
                scalar=float(-lambda_init(l)), in1=lsum.ap[:, 8 + 2 * l:8 + 2 * l + 1],
                op0=ALU.add, op1=ALU.subtract), [lsum.b], [neglam.b])

        def cast_w(dst, src, l, key, b):
            d2 = dst[l].rearrange("r (a b) -> (r a) b", b=b)
            s2 = src[l].rearrange("r (a b) -> (r a) b", b=b)
            dma("pool", d2, s2, [], [WB_b[key][l]]).nobar = True

        for l in range(n_layers):
            cast_w(wb_in, w_in_d, l, "in", 2048)
            cast_w(wb_a, w_a_d, l, "a", 1024)
            cast_w(wb_b, w_b_d, l, "b", 1024)
            cast_w(wb_out, w_out_d, l, "out", 1024)
            cast_w(wb_gu, w_gu_d, l, "gu", 1408)
            cast_w(wb_dn, w_dn_d, l, "dn", 1024)

        def layout_AC():
            ovA.reset()
            L = {}
            L["xbuf"] = [ovA.alloc([128, 8, TC], F32, nsub=8) for _ in range(2)]
            L["sqb"] = ovA.alloc([128, 8, TC], BF16, nsub=8)
            L["wbig"] = [ovA.alloc([128, 8, TC], BF16) for _ in range(4)]
            L["oac"] = ovA.alloc([128, 4, TC], BF16)
            L["obc"] = ovA.alloc([128, 4, TC], BF16)
            L["merged"] = ovA.alloc([128, 8, TC], BF16, nsub=8)
            L["M"] = ovA.alloc([128, 8, TC], F32, nsub=8)
            L["H2"] = ovA.alloc([128, 8, TC], BF16, nsub=8)
            L["ACT"] = ovA.alloc([128, NFC, TC], BF16, nsub=NFC)
            return L

        def layout_B():
            ovA.reset()
            L = {}
            L["cos"] = ovA.alloc([128, S_], F32)
            L["sin"] = ovA.alloc([128, S_], F32)
            L["QT"] = ovA.alloc([128, S_], BF16, nsub=NTC)
            L["KT"] = ovA.alloc([128, S_], BF16, nsub=NTC)
            L["VT"] = ovA.alloc([128, S_], BF16, nsub=NTC)
            L["VD"] = ovA.alloc([128, 32, 128], BF16, nsub=8)
            L["VB"] = ovA.alloc([128, 32, 2, 128], BF16, nsub=8)
            L["PT"] = [ovA.alloc([128, TC], BF16) for _ in range(4)]
            L["wsm"] = [ovA.alloc([128, 8, 128], BF16) for _ in range(6)]
            L["ACC"] = ovA.alloc([128, 2, S_], F32, nsub=2)
            L["DACC"] = [ovA.alloc([128, TC], F32) for _ in range(4)]
            return L

        LA = layout_AC()
        wi = {"n": 0}

        def x_to_xt():
            for tc in range(NTC):
                xin = LA["xbuf"][tc % 2]
                xin_ap = xin.ap.rearrange("p a b -> p (a b)").rearrange("p (a d) -> p a d", a=4)
                dma("sp", xin_ap, x_d[tc * TC:(tc + 1) * TC, :].rearrange("(a p) d -> p a d", p=128),
                    [], xin.s)
                st = LA["M"]
                for c in range(8):
                    ps = psum[c % 4]
                    for a in range(4):
                        tr(ps.ap[:, a * 128:(a + 1) * 128], xin_ap[:, a, c * 128:(c + 1) * 128], ident_f,
                           xin.s + [cst_f.b], [ps.b])
                    cp("dve" if c % 2 == 0 else "act", st.ap[:, c, :], ps.ap, [ps.b], st.s)
                dma("pool", XT_d[:, :, tc * TC:(tc + 1) * TC], st.ap, st.s, [XT_b[tc]])

        x_to_xt()

        def phase_A(l):
            for tc in range(NTC):
                tsl = slice(tc * TC, (tc + 1) * TC)
                xc = LA["xbuf"][tc % 2]
                dma("sp", xc.ap, XT_d[:, :, tsl], [XT_b[tc]], xc.s)
                sq = LA["sqb"]
                act(sq.ap, xc.ap, AF.Square, xc.s, sq.s)
                ps = psum[tc % 2]
                for c in range(8):
                    mm(ps.ap, ones_bf, sq.ap[:, c, :], c == 0, c == 7, sq.s + [cst_bf.b], [ps.b])
                rs = rot(rsr, "rs")
                rstd_from_ssq(ps, 1.0 / D_, NORM_EPS, rs)
                hb = rot(hbuf, "hb")
                for c in range(8):
                    stt(hb.ap[:, c, :], xc.ap[:, c, :], gain(0, l, c), rs.ap, ALU.mult, ALU.mult,
                        xc.s + [rs.b, gains.b], [hb.b])
                dma("pool", HT_d[:, :, tsl], hb.ap, [hb.b], [HT_b[tc]])

        def load_wsm(LB, l, col0):
            w = LB["wsm"][wi["n"] % 6]
            wi["n"] += 1
            dma("sp", w.ap, wb_in[l, :, col0:col0 + 128].rearrange("(k p) n -> p k n", p=128),
                [WB_b["in"][l]], [w.b])
            return w

        def project(LB, l, cols, dsts, ropes):
            ws = [load_wsm(LB, l, c0) for c0 in cols]
            pending = []

            def tail(ps, qb, dst, tc, k):
                tsl = slice(tc * TC, (tc + 1) * TC)
                ps2 = psum[3 + k % 2]
                mm(ps2.ap, perm_bf, qb.ap, True, True, [qb.b, cst_bf.b], [ps2.b])
                t1 = rot(tf, "tf")
                t2 = rot(tf, "tf")
                tt("dve", t1.ap, ps.ap, LB["cos"].ap[:, tsl], ALU.mult, [ps.b, LB["cos"].b, qb.b], [t1.b])
                tt("dve", t2.ap, ps2.ap, LB["sin"].ap[:, tsl], ALU.mult, [ps2.b, LB["sin"].b], [t2.b])
                tt("pool", dst.ap[:, tsl], t1.ap, t2.ap, ALU.add, [t1.b, t2.b], [dst.s[tc]])

            k = 0
            for tc in range(NTC):
                tsl = slice(tc * TC, (tc + 1) * TC)
                hc = rot(hbuf, "hb")
                dma("sp", hc.ap, HT_d[:, :, tsl], [HT_b[tc]], [hc.b])
                for j, (w, dst, rope) in enumerate(zip(ws, dsts, ropes)):
                    ps = psum[j % 3]
                    for kc in range(8):
                        mm(ps.ap, w.ap[:, kc, :], hc.ap[:, kc, :], kc == 0, kc == 7, [w.b, hc.b], [ps.b])
                    if rope:
                        qb = rot(tb, "tb")
                        cp("act", qb.ap, ps.ap, [ps.b], [qb.b])
                        new_p = (ps, qb, dst, tc, k)
                        k += 1
                    else:
                        cp("act", dst.ap[:, tsl], ps.ap, [ps.b], [dst.s[tc]])
                        new_p = None
                    while pending:
                        tail(*pending.pop(0))
                    if new_p is not None:
                        pending.append(new_p)
            while pending:
                tail(*pending.pop(0))

        def v_transposes(LB, d, dil_mode):
            VT = LB["VT"]
            nkc = 32 // d
            psb = [psum[5].ap.bitcast(BF16), psum[6].ap.bitcast(BF16)]
            for g4 in range(8):
                ps = psum[5 + g4 % 2]
                pv = psb[g4 % 2]
                for a in range(4):
                    ch = g4 * 4 + a
                    r, c = ch // nkc, ch % nkc
                    t0 = r + d * 128 * c
                    src = VT.ap[:, t0:t0 + 127 * d + 1:d] if d > 1 else VT.ap[:, t0:t0 + 128]
                    tr(pv[:, a * 128:(a + 1) * 128], src, ident_bf, VT.s + [cst_bf.b], [ps.b])
                if dil_mode:
                    VB = LB["VB"]
                    cp("dve", VB.ap[:, g4 * 4:(g4 + 1) * 4, :, 0:64],
                       pv[:, 0:512].rearrange("p (a h e) -> p a h e", a=4, h=2), [ps.b], [VB.s[g4]])
                else:
                    VD = LB["VD"]
                    cp("dve", VD.ap[:, g4 * 4:(g4 + 1) * 4, :],
                       pv[:, 0:512].rearrange("p (a e) -> p a e", a=4), [ps.b], [VD.s[g4]])

        def diff_head(LB, l, hd):
            QT, KT, VD = LB["QT"], LB["KT"], LB["VD"]
            project(LB, l, [128 * hd, 512 + 128 * hd, 1024 + 128 * hd], [QT, KT, LB["VT"]], [True, True, False])
            if upto == "B1":
                return
            v_transposes(LB, 1, False)
            if upto == "B2":
                return
            st_ring = [psum[0], psum[1], psum[2], psum[3]]
            num = [psum[4], psum[5]]
            den = [psum[6], psum[7]]
            ssq_ps = psum[7]
            PT = LB["PT"]
            DACC = LB["DACC"]
            ones_f = cst_f.ap[:, 512:640]
            pairs = [(qc, kc) for qc in range(NTC) for kc in range(32)]
            npairs = len(pairs)

            def QKp(p):
                qc, kc = pairs[p]
                for c in range(2):
                    st = st_ring[(2 * p + c) % 4]
                    mm(st.ap, KT.ap[64 * c:64 * c + 64, kc * 128:(kc + 1) * 128],
                       QT.ap[64 * c:64 * c + 64, qc * TC:(qc + 1) * TC], True, True, [KT.s[kc // 4], QT.s[qc]], [st.b])
                for c in range(2):
                    st = st_ring[(2 * p + c) % 4]
                    pt = PT[(2 * p + c) % 4]
                    act(pt.ap, st.ap, AF.Exp, [st.b], [pt.b], scale=0.125)

            def PVp(p):
                qc, kc = pairs[p]
                for c in range(2):
                    pt = PT[(2 * p + c) % 4]
                    mm(num[c].ap, VD.ap[:, kc, :], pt.ap, kc == 0, kc == 31, [VD.s[kc // 4], pt.b], [num[c].b])
                    acc = DACC[2 * (qc % 2) + c]
                    if kc == 0:
                        cp("dve", acc.ap, pt.ap, [pt.b], [acc.b])
                    else:
                        tt("dve", acc.ap, acc.ap, pt.ap, ALU.add, [acc.b, pt.b], [acc.b])

            keep = {}

            def fin_num(qc):
                nm = [rot(tf, "tf") for _ in range(2)]
                for c in range(2):
                    cp("dve", nm[c].ap, num[c].ap, [num[c].b], [nm[c].b])
                keep[qc] = [nm]

            def fin_a(qc):
                nm = keep[qc][0]
                dn = [rot(tf, "tf") for _ in range(2)]
                for c in range(2):
                    acc = DACC[2 * (qc % 2) + c]
                    mm(den[c].ap, ones_f, acc.ap, True, True, [acc.b, cst_f.b], [den[c].b])
                for c in range(2):
                    recip(dn[c].ap, den[c].ap, [den[c].b], [dn[c].b])
                for c in range(2):
                    tt("dve", nm[c].ap, nm[c].ap, dn[c].ap, ALU.mult, [nm[c].b, dn[c].b], [nm[c].b])
                a_ = rot(tf, "tf")
                stt(a_.ap, nm[1].ap, neglam.ap[:, l:l + 1], nm[0].ap, ALU.mult, ALU.add,
                    [nm[0].b, nm[1].b, neglam.b], [a_.b])
                sq = rot(tb, "tb")
                tt("pool", sq.ap, a_.ap, a_.ap, ALU.mult, [a_.b], [sq.b])
                keep[qc] = (a_, sq)

            def fin_b(qc):
                a_, sq = keep.pop(qc)
                mm(ssq_ps.ap, ones_bf, sq.ap, True, True, [sq.b, cst_bf.b], [ssq_ps.b])
                rs = rot(rsr, "rs")
                rstd_from_ssq(ssq_ps, 1.0 / 128, SUBLN_EPS, rs)
                ob = rot(obuf, "ob")
                stt(ob.ap, a_.ap, gsub.ap[:, l:l + 1], rs.ap, ALU.mult, ALU.mult, [a_.b, rs.b, gsub.b], [ob.b])
                dma("pool", OA_d[:, hd, qc * TC:(qc + 1) * TC], ob.ap, [ob.b], [OA_b[hd][qc]])

            pend = []
            QKp(0)
            for p in range(npairs):
                if p + 1 < npairs:
                    QKp(p + 1)
                PVp(p)
                qc, kc = pairs[p]
                if kc == 31:
                    fin_num(qc)
                    pend.append((p + 2, fin_a, qc))
                    pend.append((p + 12, fin_b, qc))
                    pend.sort(key=lambda t: t[0])
                while pend and pend[0][0] <= p:
                    _, f, q_ = pend.pop(0)
                    f(q_)
            while pend:
                _, f, q_ = pend.pop(0)
                f(q_)

        def dil_pair(LB, l, j):
            QT, KT, VB, ACC = LB["QT"], LB["KT"], LB["VB"], LB["ACC"]
            wstate = {"n": 0}
            for g, d in enumerate(DIL):
                base = 1536 + 1536 * g + 128 * j
                project(LB, l, [base, base + 512, base + 1024], [QT, KT, LB["VT"]], [True, True, False])
                v_transposes(LB, d, True)
                Lg = S_ // d
                nkc = Lg // 128
                items = [(h, r, c) for h in range(2) for r in range(d) for c in range(nkc)]
                n = len(items)
                LA = 2
                PT = LB["PT"]
                st_ring = [psum[0], psum[1], psum[2]]
                wps = [None, None]

                def toks(r, p0, p1, d=d):
                    return slice(r + d * p0, r + d * (p1 - 1) + 1, d) if d > 1 else slice(p0, p1)

                def geom(c, Lg=Lg):
                    q0 = max(0, 128 * c - 64)
                    q1 = min(Lg, 128 * c + 192)
                    return q0, q1

                def QK(i):
                    h, r, c = items[i]
                    rows = slice(64 * h, 64 * h + 64)
                    q0, q1 = geom(c)
                    nq = q1 - q0
                    moff = q0 - (128 * c - 64)
                    st = st_ring[i % 3]
                    mm(st.ap[:, 0:nq], KT.ap[rows, toks(r, 128 * c, 128 * c + 128)], QT.ap[rows, toks(r, q0, q1)],
                       True, False, KT.s + QT.s, [st.b])
                    mm(st.ap[:, 0:nq], ident_bf, mask_bf[:, moff:moff + nq], False, True, [cst_bf.b], [st.b])
                    pt = PT[i % 4]
                    act(pt.ap[:, 0:nq], st.ap[:, 0:nq], AF.Exp, [st.b], [pt.b], scale=0.125)

                def PV(i):
                    h, r, c = items[i]
                    q0, q1 = geom(c)
                    nq = q1 - q0
                    pt = PT[i % 4]
                    ch = r * nkc + c
                    va = VB.ap[:, ch, h, :]
                    split = 128 * c + 64 - q0
                    for w, (c0, c1) in ((c, (0, split)), (c + 1, (split, nq))):
                        first = (w == c + 1) or (c == 0)
                        last = (w == c) or (c == nkc - 1)
                        if first:
                            wps[w % 2] = psum[3 + wstate["n"] % 4]
                            wstate["n"] += 1
                        wp = wps[w % 2]
                        mm(wp.ap[:, 0:c1 - c0], va, pt.ap[:, c0:c1], first, last, [VB.s[ch // 4], pt.b], [wp.b])
                        if last:
                            p0 = max(0, 128 * w - 64)
                            p1 = min(Lg, 128 * w + 64)
                            assert p1 - p0 == c1 - c0, (p0, p1, c0, c1)
                            dst = ACC.ap[:, h, toks(r, p0, p1)]
                            if g == 0:
                                cp("dve", dst, wp.ap[:, 0:p1 - p0], [wp.b], [ACC.s[h]])
                            else:
                                tt("dve", dst, wp.ap[:, 0:p1 - p0], dst, ALU.add, [wp.b, ACC.s[h]], [ACC.s[h]])

                for i in range(min(LA, n)):
                    QK(i)
                for i in range(n):
                    if i + LA < n:
                        QK(i + LA)
                    PV(i)
            for tc in range(NTC):
                tsl = slice(tc * TC, (tc + 1) * TC)
                ob = rot(obuf, "ob")
                for h in range(2):
                    rd = rot(tf, "tf")
                    recip(rd.ap[0:64, :], ACC.ap[64:128, h, tsl], [ACC.s[h]], [rd.b])
                    tt("dve", ob.ap[64 * h:64 * h + 64, :], ACC.ap[0:64, h, tsl], rd.ap[0:64, :], ALU.mult,
                       [ACC.s[h], rd.b], [ob.b])
                dma("pool", OB_d[:, j, tsl], ob.ap, [ob.b], [OB_b[j][tc]])

        def phase_B(l, LB):
            for hd in range(nheads[0]):
                mark("diff%d.%d" % (l, hd))
                diff_head(LB, l, hd)
            for j in range(nheads[1]):
                mark("dil%d.%d" % (l, j))
                dil_pair(LB, l, j)

        def load_wbig(src_ap, kcs, buf):
            w = LA["wbig"][wi["n"] % 4]
            wi["n"] += 1
            dma("sp", w.ap[:, 0:kcs, :], src_ap.rearrange("(k p) n -> p k n", p=128), [buf], [w.b])
            return w

        def sq_chunk(M, j, ssq_ps):
            sq = LA["sqb"]
            act(sq.ap[:, j, :], M.ap[:, j, :], AF.Square, [M.s[j]], [sq.s[j]])
            mm(ssq_ps.ap, ones_bf, sq.ap[:, j, :], j == 0, j == 7, [sq.s[j], cst_bf.b], [ssq_ps.b])

        def norm_residual(l, ty, xc, ssq_ps, after_chunk=None):
            M = LA["M"]
            rs = rot(rsr, "rs")
            rstd_from_ssq(ssq_ps, 1.0 / D_, NORM_EPS, rs)
            for c in range(8):
                t = rot(tf, "tf")
                stt(t.ap, M.ap[:, c, :], gain(ty, l, c), rs.ap, ALU.mult, ALU.mult, [M.s[c], rs.b, gains.b], [t.b])
                tt("dve" if c % 2 == 0 else "pool", xc.ap[:, c, :], xc.ap[:, c, :], t.ap, ALU.add,
                   [xc.s[c], t.b], [xc.s[c]])
                if after_chunk is not None:
                    after_chunk(c)

        def phase_C(l):
            pr = {"n": 0}

            def nps():
                p = psum[pr["n"] % 6]
                pr["n"] += 1
                return p

            for tc in range(NTC):
                tsl = slice(tc * TC, (tc + 1) * TC)
                xc = LA["xbuf"][tc % 2]
                dma("sp", xc.ap, XT_d[:, :, tsl], [XT_b[tc]], xc.s)
                hc = rot(hbuf, "hb")
                dma("sp", hc.ap, HT_d[:, :, tsl], [HT_b[tc]], [hc.b])
                oac, obc = LA["oac"], LA["obc"]
                dma("sp", oac.ap, OA_d[:, :, tsl], [OA_b[h][tc] for h in range(4)], [oac.b])
                dma("sp", obc.ap, OB_d[:, :, tsl], [OB_b[h][tc] for h in range(4)], [obc.b])
                merged = LA["merged"]
                for half in range(2):
                    wgA = load_wbig(wb_in[l, :, 6144 + 512 * half:6144 + 512 * half + 512], 8, WB_b["in"][l])
                    wgB = load_wbig(wb_in[l, :, 7168 + 512 * half:7168 + 512 * half + 512], 8, WB_b["in"][l])
                    wa = load_wbig(wb_a[l, :, 512 * half:512 * half + 512], 4, WB_b["a"][l])
                    wb_ = load_wbig(wb_b[l, :, 512 * half:512 * half + 512], 4, WB_b["b"][l])
                    for jj in range(4):
                        j = 4 * half + jj
                        js = slice(128 * jj, 128 * jj + 128)
                        parts = []
                        for (wg, wp_, oc) in ((wgA, wa, oac), (wgB, wb_, obc)):
                            pg = nps()
                            for kc in range(8):
                                mm(pg.ap, wg.ap[:, kc, js], hc.ap[:, kc, :], kc == 0, kc == 7, [wg.b, hc.b], [pg.b])
                            sg = rot(tf, "tf")
                            act(sg.ap, pg.ap, AF.Sigmoid, [pg.b], [sg.b])
                            pp = nps()
                            for kc in range(4):
                                mm(pp.ap, wp_.ap[:, kc, js], oc.ap[:, kc, :], kc == 0, kc == 3, [wp_.b, oc.b], [pp.b])
                            t = rot(tf, "tf")
                            tt("dve", t.ap, pp.ap, sg.ap, ALU.mult, [pp.b, sg.b], [t.b])
                            parts.append(t)
                        tt("pool", merged.ap[:, j, :], parts[0].ap, parts[1].ap, ALU.add,
                           [parts[0].b, parts[1].b], [merged.s[j]])
                M = LA["M"]
                ssq1, ssq2 = psum[7], psum[6]
                for half in range(2):
                    wo = load_wbig(wb_out[l, :, 512 * half:512 * half + 512], 8, WB_b["out"][l])
                    for jj in range(4):
                        j = 4 * half + jj
                        pm = nps()
                        for kc in range(8):
                            mm(pm.ap, wo.ap[:, kc, 128 * jj:128 * jj + 128], merged.ap[:, kc, :], kc == 0, kc == 7,
                               [wo.b] + merged.s, [pm.b])
                        cp("act" if j % 2 else "dve", M.ap[:, j, :], pm.ap, [pm.b], [M.s[j]])
                        if j >= 1:
                            sq_chunk(M, j - 1, ssq1)
                sq_chunk(M, 7, ssq1)
                if upto == "C0b":
                    dma("pool", XT_d[:, :, tsl], M.ap, M.s, [XT_b[tc]])
                    continue
                norm_residual(l, 1, xc, ssq1, after_chunk=lambda c: sq_chunk(xc, c, ssq2))
                if upto == "C1":
                    dma("pool", XT_d[:, :, tsl], xc.ap, xc.s, [XT_b[tc]])
                    continue
                rs = rot(rsr, "rs")
                rstd_from_ssq(ssq2, 1.0 / D_, NORM_EPS, rs)
                H2 = LA["H2"]
                for c in range(8):
                    stt(H2.ap[:, c, :], xc.ap[:, c, :], gain(2, l, c), rs.ap, ALU.mult, ALU.mult,
                        [xc.s[c], rs.b, gains.b], [H2.s[c]])
                ACTT = LA["ACT"]
                for i4 in range(0, NFC, 4):
                    n4 = min(4, NFC - i4)
                    wg = LA["wbig"][wi["n"] % 4]
                    wi["n"] += 1
                    dma("sp", wg.ap[:, :, 0:128 * n4],
                        wb_gu[l, :, 128 * i4:128 * (i4 + n4)].rearrange("(k p) n -> p k n", p=128),
                        [WB_b["gu"][l]], [wg.b])
                    wu = LA["wbig"][wi["n"] % 4]
                    wi["n"] += 1
                    dma("sp", wu.ap[:, :, 0:128 * n4],
                        wb_gu[l, :, FF + 128 * i4:FF + 128 * (i4 + n4)].rearrange("(k p) n -> p k n", p=128),
                        [WB_b["gu"][l]], [wu.b])
                    for ii in range(n4):
                        i = i4 + ii
                        isl = slice(128 * ii, 128 * ii + 128)
                        pg = nps()
                        for kc in range(8):
                            mm(pg.ap, wg.ap[:, kc, isl], H2.ap[:, kc, :], kc == 0, kc == 7, [wg.b] + H2.s, [pg.b])
                        pu = nps()
                        for kc in range(8):
                            mm(pu.ap, wu.ap[:, kc, isl], H2.ap[:, kc, :], kc == 0, kc == 7, [wu.b] + H2.s, [pu.b])
                        sg = rot(tf, "tf")
                        act(sg.ap, pg.ap, AF.Silu, [pg.b], [sg.b])
                        tt("dve", ACTT.ap[:, i, :], pu.ap, sg.ap, ALU.mult, [pu.b, sg.b], [ACTT.s[i]])
                for half in range(2):
                    pms = [nps() for _ in range(4)]
                    for (k0, kn) in ((0, 8), (8, 8), (16, 6)):
                        wd = load_wbig(wb_dn[l, 128 * k0:128 * (k0 + kn), 512 * half:512 * half + 512], kn,
                                       WB_b["dn"][l])
                        for jj in range(4):
                            for kk in range(kn):
                                kc = k0 + kk
                                mm(pms[jj].ap, wd.ap[:, kk, 128 * jj:128 * jj + 128], ACTT.ap[:, kc, :],
                                   kc == 0, kc == NFC - 1, [wd.b, ACTT.s[kc]], [pms[jj].b])
                        if half == 1 and k0 == 0:
                            for jj in range(4):
                                sq_chunk(M, jj, ssq1)
                    for jj in range(4):
                        j = 4 * half + jj
                        cp("act" if j % 2 else "dve", M.ap[:, j, :], pms[jj].ap, [pms[jj].b], [M.s[j]])
                for jj in range(4, 8):
                    sq_chunk(M, jj, ssq1)
                norm_residual(l, 3, xc, ssq1)
                dma("pool", XT_d[:, :, tsl], xc.ap, xc.s, [XT_b[tc]])

        def xt_to_out():
            for tc in range(NTC):
                tsl = slice(tc * TC, (tc + 1) * TC)
                xc = LA["xbuf"][tc % 2]
                dma("sp", xc.ap, XT_d[:, :, tsl], [XT_b[tc]], xc.s)
                st = LA["M"]
                st_ap = st.ap.rearrange("p a b -> p (a b)").rearrange("p (a d) -> p a d", a=4)
                for a in range(4):
                    for half in range(2):
                        ps = psum[(2 * a + half) % 4]
                        for cc in range(4):
                            c = 4 * half + cc
                            tr(ps.ap[:, cc * 128:(cc + 1) * 128], xc.ap[:, c, a * 128:(a + 1) * 128], ident_f,
                               xc.s + [cst_f.b], [ps.b])
                        cp("dve" if half else "act", st_ap[:, a, 512 * half:512 * half + 512], ps.ap, [ps.b], st.s)
                dma("pool", out_d[tc * TC:(tc + 1) * TC, :].rearrange("(a p) d -> p a d", p=128), st_ap,
                    st.s, [out_bs[tc]])

        S.marks = []
        mark = lambda nm: S.marks.append((nm, len(S.prog["pe"])))
        for l in range(n_layers):
            if upto == "prologue":
                break
            mark("A%d" % l)
            phase_A(l)
            if upto == "A":
                break
            S.barrier()
            LB = layout_B()
            dma("sp", LB["cos"].ap, cos_d, [], [LB["cos"].b])
            dma("sp", LB["sin"].ap, sin_d, [], [LB["sin"].b])
            for ch in range(32):
                for h in range(2):
                    S.add("pool", lambda e, ch=ch, h=h, vb=LB["VB"]: e.memset(vb.ap[:, ch, h, 64:128], 1.0),
                          [], [LB["VB"].s[ch // 4]])
            if upto == "B0":
                break
            phase_B(l, LB)
            if upto in ("B1", "B2"):
                break
            S.barrier()
            LA = layout_AC()
            if upto == "B":
                break
            mark("C%d" % l)
            phase_C(l)
        mark("epi")
        xt_to_out()
        mark("end")
        build.marks = S.marks
        if dbg:
            for name, shape, dt in dbg:
                src = {"XT": XT_d, "HT": HT_d, "OAT": OA_d, "OBT": OB_d}[name]
                dma("sp", dbg_d[name], src, sum([XT_b, HT_b] + OA_b + OB_b, []), [out_bs[NTC + len(dbg_d) % 4]])
        S.add("sp", None, out_bs, [])
        S.emit(nc, block, sems, rings)
    return nc


def _rope_tables():
    inv = (1.0 / (np.float32(10000.0) ** (np.arange(0, 64, 2, dtype=np.float32) / np.float32(64)))).astype(np.float32)
    ang = (np.arange(S_, dtype=np.float32)[:, None] * inv[None, :]).astype(np.float32)
    cos = np.cos(ang).astype(np.float32).T
    sin = np.sin(ang).astype(np.float32).T
    cos2 = np.concatenate([cos, cos, cos, cos], axis=0)
    ssin = np.concatenate([-sin, sin, -sin, sin], axis=0)
    return np.ascontiguousarray(cos2), np.ascontiguousarray(ssin)


def _consts():
    c = np.zeros((128, 640), np.float32)
    c[:, 0:128] = np.eye(128, dtype=np.float32)
    for m in range(128):
        k = m + 32 if (m % 64) < 32 else m - 32
        c[k, 128 + m] = 1.0
    i = np.arange(128)[:, None]
    jj = np.arange(256)[None, :]
    c[:, 256:512] = np.where((jj >= i) & (jj <= i + 128), 0.0, MASK_NEG)
    c[:, 512:640] = 1.0
    return c


_NC_CACHE = {}


def make_inputs(x, w_in, w_a, w_b, w_out, lam, subln_g, norm_mix_pre, norm_mix_post,
                w_gate_up, w_down, norm_ffn_pre, norm_ffn_post):
    f = lambda a: np.ascontiguousarray(np.asarray(a, dtype=np.float32))
    cos2, ssin = _rope_tables()
    norms = np.stack([f(norm_mix_pre), f(norm_mix_post), f(norm_ffn_pre), f(norm_ffn_post)], 0)
    norms = np.ascontiguousarray(norms.reshape(4 * DEPTH * 8, 128))
    shared = {
        "w_in": f(w_in), "w_a": f(w_a), "w_b": f(w_b), "w_out": f(w_out),
        "lam": f(lam).reshape(1, -1), "subln_g": f(subln_g), "norms": norms,
        "w_gate_up": f(w_gate_up), "w_down": f(w_down),
        "cos2": cos2, "ssin": ssin, "consts": _consts(),
    }
    x = f(x)
    return [dict(shared, x=x[i]) for i in range(N_CORES)]


def kernel(**inputs):
    in_maps = make_inputs(**inputs)
    if "nc" not in _NC_CACHE:
        _NC_CACHE["nc"] = build()
    res = run_bass_kernel_spmd(_NC_CACHE["nc"], in_maps, core_ids=list(range(N_CORES)))
    return np.stack([np.asarray(r["out"], dtype=np.float32) for r in res.results], axis=0)
```

```python
import contextlib
import math
import numpy as np
import concourse.bass as bass
import concourse.mybir as mybir
from concourse.bass_utils import run_bass_kernel_spmd

F32 = mybir.dt.float32
BF16 = mybir.dt.bfloat16
U8 = mybir.dt.uint8
ALU = mybir.AluOpType
AF = mybir.ActivationFunctionType
AX = mybir.AxisListType

S_ = 4096
D_ = 1024
DEPTH = 4
TC = 512
NTC = S_ // TC
FF = 2816
NFC = FF // 128
IN_COLS = 8192
DIL = (1, 4, 16)
NORM_EPS = 1e-6
SUBLN_EPS = 1e-5
MASK_NEG = -30000.0
N_CORES = 8
RING = 8
RINGS = {"sp": 8, "pool": 2}


class Buf:
    __slots__ = ("w", "r", "excl")

    def __init__(self, excl=False):
        self.w = None
        self.r = {}
        self.excl = excl


class Op:
    __slots__ = ("eng", "fn", "deps", "seq", "sig", "sigval", "dma", "dman", "waits", "nobar")


class Sched:
    ENG = ("pe", "act", "dve", "pool", "sp")

    def __init__(self):
        self.prog = {e: [] for e in self.ENG}
        self.ndma = {e: 0 for e in self.ENG}
        self.dma_since_bar = []

    def add(self, eng, fn, reads=(), writes=(), dma=False, extra=()):
        op = Op()
        op.eng = eng
        op.fn = fn
        op.dma = dma
        op.sig = False
        op.sigval = 0
        op.dman = -1
        op.nobar = False
        deps = {}
        for b in reads:
            if b.w is not None:
                deps[id(b.w)] = b.w
            if b.excl:
                for r in b.r.values():
                    if r.eng != eng:
                        deps[id(r)] = r
        for b in writes:
            if b.w is not None:
                deps[id(b.w)] = b.w
            for r in b.r.values():
                deps[id(r)] = r
        for d in extra:
            deps[id(d)] = d
        op.deps = list(deps.values())
        op.seq = len(self.prog[eng])
        self.prog[eng].append(op)
        if dma:
            op.dman = self.ndma[eng]
            self.ndma[eng] += 1
            self.dma_since_bar.append(op)
        for b in reads:
            b.r[(eng, op.seq) if dma else eng] = op
        for b in writes:
            b.w = op
            b.r = {}
        return op

    def barrier(self):
        last = []
        for e in self.ENG:
            for op in reversed(self.prog[e]):
                if op.fn is not None and not op.dma:
                    last.append(op)
                    break
        bar = self.add("sp", self.bar_fn, extra=last + [d for d in self.dma_since_bar if not d.nobar], dma=True)
        self.dma_since_bar = []
        for e in self.ENG:
            if e != "sp":
                self.add(e, None, extra=[bar])
        return bar

    def emit(self, nc, block, sems, rings):
        for eng in self.ENG:
            waited = {}
            waited_dma = set()
            for op in self.prog[eng]:
                op.waits = []
                best = {}
                for d in op.deps:
                    if d.dma:
                        if id(d) not in waited_dma:
                            waited_dma.add(id(d))
                            op.waits.append(d)
                    else:
                        if d.eng == "pe" and eng == "pe":
                            continue
                        if d.eng not in best or best[d.eng].seq < d.seq:
                            best[d.eng] = d
                for pe_, d in best.items():
                    if waited.get(pe_, -1) >= d.seq:
                        continue
                    waited[pe_] = d.seq
                    d.sig = True
                    op.waits.append(d)
        for eng in self.ENG:
            cnt = 0
            for op in self.prog[eng]:
                if not op.dma and op.sig:
                    cnt += 1
                    op.sigval = cnt
        hook = {"pe": block.tensor, "act": block.scalar, "dve": block.vector,
                "pool": block.gpsimd, "sp": block.sync}

        def semval(d):
            if d.dma:
                return rings[d.eng][d.dman % RINGS[d.eng]], 16 * (d.dman // RINGS[d.eng] + 1)
            return sems[d.eng], d.sigval

        for eng in self.ENG:
            ops = self.prog[eng]

            def body(e, ops=ops, eng=eng):
                for op in ops:
                    for d in op.waits:
                        s, v = semval(d)
                        e.wait_ge(s, v)
                    if op.dma:
                        n = op.dman
                        R_ = RINGS[eng]
                        if n >= R_:
                            e.wait_ge(rings[eng][n % R_], 16 * (n // R_))
                        ins = op.fn(e)
                        ins.then_inc(rings[eng][n % R_], 16)
                    elif op.fn is not None:
                        ins = op.fn(e)
                        if op.sig:
                            ins.then_inc(sems[eng], 1)
                    else:
                        assert not op.sig

            hook[eng](body)


class T:
    def __init__(self, ap, nsub=0):
        self.ap = ap
        self.b = Buf()
        self.s = [Buf() for _ in range(nsub)]


class Arena:
    def __init__(self, base_ap, nbytes):
        self.base = base_ap
        self.nbytes = nbytes
        self.off = 0

    def reset(self):
        self.off = 0

    def alloc(self, shape, dtype, nsub=0):
        esz = 4 if dtype == F32 else 2
        n = int(np.prod(shape[1:]))
        nb = n * esz
        nb_al = (nb + 63) // 64 * 64
        assert self.off + nb_al <= self.nbytes, (self.off, nb_al, self.nbytes)
        ap = self.base[:, self.off:self.off + nb].bitcast(dtype)
        self.off += nb_al
        if len(shape) == 3:
            ap = ap.rearrange("p (a b) -> p a b", a=shape[1])
        elif len(shape) == 4:
            ap = ap.rearrange("p (a b c) -> p a b c", a=shape[1], b=shape[2])
        return T(ap, nsub)


def lambda_init(l):
    return 0.8 - 0.6 * math.exp(-0.3 * l)


def build(n_layers=DEPTH, dbg=None, upto=None, nheads=(4, 4), WD=DEPTH):
    nc = bass.Bass("TRN2", target_bir_lowering=False)
    S = Sched()

    def dram(name, shape, dt, kind):
        return nc.dram_tensor(name, list(shape), dt, kind=kind).ap()

    x_d = dram("x", [S_, D_], F32, "ExternalInput")
    w_in_d = dram("w_in", [WD, D_, IN_COLS], F32, "ExternalInput")
    w_a_d = dram("w_a", [WD, 512, D_], F32, "ExternalInput")
    w_b_d = dram("w_b", [WD, 512, D_], F32, "ExternalInput")
    w_out_d = dram("w_out", [WD, D_, D_], F32, "ExternalInput")
    lam_d = dram("lam", [1, DEPTH * 4 * 64], F32, "ExternalInput")
    subln_d = dram("subln_g", [DEPTH, 128], F32, "ExternalInput")
    norms_d = dram("norms", [4 * DEPTH * 8, 128], F32, "ExternalInput")
    w_gu_d = dram("w_gate_up", [WD, D_, 2 * FF], F32, "ExternalInput")
    w_dn_d = dram("w_down", [WD, FF, D_], F32, "ExternalInput")
    cos_d = dram("cos2", [128, S_], F32, "ExternalInput")
    sin_d = dram("ssin", [128, S_], F32, "ExternalInput")
    cst_d = dram("consts", [128, 640], F32, "ExternalInput")
    out_d = dram("out", [S_, D_], F32, "ExternalOutput")

    XT_d = dram("XT", [128, 8, S_], F32, "Internal")
    HT_d = dram("HT", [128, 8, S_], BF16, "Internal")
    OA_d = dram("OAT", [128, 4, S_], BF16, "Internal")
    OB_d = dram("OBT", [128, 4, S_], BF16, "Internal")
    wb_in = dram("wb_in", [DEPTH, D_, IN_COLS], BF16, "Internal")
    wb_a = dram("wb_a", [DEPTH, 512, D_], BF16, "Internal")
    wb_b = dram("wb_b", [DEPTH, 512, D_], BF16, "Internal")
    wb_out = dram("wb_out", [DEPTH, D_, D_], BF16, "Internal")
    wb_gu = dram("wb_gu", [DEPTH, D_, 2 * FF], BF16, "Internal")
    wb_dn = dram("wb_dn", [DEPTH, FF, D_], BF16, "Internal")
    bar_d = dram("bar_scratch", [2, 64], F32, "Internal")
    S.bar_fn = lambda e: e.dma_start(out=bar_d[1:2, :], in_=cst_d[0:1, 0:64])
    dbg_d = {}
    if dbg:
        for name, shape, dt in dbg:
            dbg_d[name] = dram("dbg_" + name, shape, dt, "ExternalOutput")

    XT_b = [Buf() for _ in range(NTC)]
    HT_b = [Buf() for _ in range(NTC)]
    OA_b = [[Buf() for _ in range(NTC)] for _ in range(4)]
    OB_b = [[Buf() for _ in range(NTC)] for _ in range(4)]
    WB_b = {k: [Buf() for _ in range(DEPTH)] for k in ("in", "a", "b", "out", "gu", "dn")}
    out_bs = [Buf() for _ in range(NTC + 4)]

    es = contextlib.ExitStack()
    with es:
        sems = {e: es.enter_context(nc.semaphore("sem_" + e)) for e in Sched.ENG}
        rings = {e: [es.enter_context(nc.semaphore("ring_%s%d" % (e, i))) for i in range(RING)]
                 for e in ("sp", "pool")}
        SHARED = 50 * 1024
        OVL = 140 * 1024
        sh_t = es.enter_context(nc.sbuf_tensor("shared", [128, SHARED], U8))
        ov_t = es.enter_context(nc.sbuf_tensor("overlay", [128, OVL], U8))
        shA = Arena(sh_t[:], SHARED)
        ovA = Arena(ov_t[:], OVL)
        psum = []
        for i in range(8):
            pt_ = es.enter_context(nc.psum_tensor("ps%d" % i, [128, 512], F32))
            psum.append(T(pt_[:]))
            psum[-1].b.excl = True
        block = es.enter_context(nc.Block())

        cst_f = shA.alloc([128, 640], F32)
        ident_f = cst_f.ap[:, 0:128]
        cst_bf = shA.alloc([128, 640], BF16)
        ident_bf = cst_bf.ap[:, 0:128]
        perm_bf = cst_bf.ap[:, 128:256]
        mask_bf = cst_bf.ap[:, 256:512]
        ones_bf = cst_bf.ap[:, 512:640]
        gains = shA.alloc([128, 128], F32)
        gsub = shA.alloc([128, 4], F32)
        neglam = shA.alloc([128, 4], F32)
        hbuf = [shA.alloc([128, 8, TC], BF16) for _ in range(2)]
        tf = [shA.alloc([128, TC], F32) for _ in range(8)]
        tb = [shA.alloc([128, TC], BF16) for _ in range(4)]
        obuf = [shA.alloc([128, TC], BF16) for _ in range(2)]
        rsr = [shA.alloc([128, TC], F32) for _ in range(2)]
        rr = {"tf": 0, "tb": 0, "ob": 0, "hb": 0, "rs": 0}

        def rot(lst, key):
            t = lst[rr[key] % len(lst)]
            rr[key] += 1
            return t

        def gain(ty, l, c):
            i = (ty * DEPTH + l) * 8 + c
            return gains.ap[:, i:i + 1]

        def mm(out, lhsT, rhs, start, stop, reads, writes):
            S.add("pe", lambda e: e.matmul(out, lhsT=lhsT, rhs=rhs, start=start, stop=stop), reads, writes)

        def tr(out, in_, ident, reads, writes):
            S.add("pe", lambda e: e.transpose(out=out, in_=in_, identity=ident), reads, writes)

        def act(out, in_, func, reads, writes, scale=None, bias=None):
            kw = {}
            if scale is not None:
                kw["scale"] = scale
            if bias is not None:
                kw["bias"] = bias
            S.add("act", lambda e: e.activation(out=out, in_=in_, func=func, **kw), reads, writes)

        def tt(eng, out, in0, in1, op, reads, writes):
            S.add(eng, lambda e: e.tensor_tensor(out=out, in0=in0, in1=in1, op=op), reads, writes)

        def stt(out, in0, scalar, in1, op0, op1, reads, writes):
            S.add("dve", lambda e: e.scalar_tensor_tensor(out=out, in0=in0, scalar=scalar, in1=in1, op0=op0, op1=op1),
                  reads, writes)

        def cp(eng, out, in_, reads, writes):
            if eng == "act":
                S.add("act", lambda e: e.copy(out=out, in_=in_), reads, writes)
            else:
                S.add(eng, lambda e: e.tensor_copy(out=out, in_=in_), reads, writes)

        def recip(out, in_, reads, writes):
            S.add("dve", lambda e: e.reciprocal(out=out, in_=in_), reads, writes)

        def dma(q, out, in_, reads, writes, **kw):
            return S.add(q, lambda e: e.dma_start(out=out, in_=in_, **kw), reads, writes, dma=True)

        def rstd_from_ssq(ssq_ps, inv_n, eps, dst):
            t = rot(tf, "tf")
            act(t.ap, ssq_ps.ap, AF.Ln, [ssq_ps.b], [t.b], scale=inv_n, bias=eps_t[eps])
            act(dst.ap, t.ap, AF.Exp, [t.b], [dst.b], scale=-0.5)

        eps_tile = shA.alloc([128, 2], F32)
        eps_t = {NORM_EPS: eps_tile.ap[:, 0:1], SUBLN_EPS: eps_tile.ap[:, 1:2]}
        S.add("dve", lambda e: e.memset(eps_tile.ap[:, 0:1], NORM_EPS), [], [eps_tile.b])
        S.add("dve", lambda e: e.memset(eps_tile.ap[:, 1:2], SUBLN_EPS), [], [eps_tile.b])
        dma("sp", cst_f.ap, cst_d, [], [cst_f.b])
        cp("dve", cst_bf.ap, cst_f.ap, [cst_f.b], [cst_bf.b])
        gin = rot(tf, "tf")
        dma("sp", gin.ap[:, 0:128], norms_d, [], [gin.b])
        tr(psum[0].ap[:, 0:128], gin.ap[:, 0:128], ident_f, [gin.b, cst_f.b], [psum[0].b])
        cp("dve", gains.ap, psum[0].ap[:, 0:128], [psum[0].b], [gains.b])
        gin2 = rot(tf, "tf")
        dma("sp", gin2.ap[0:4, 0:128], subln_d, [], [gin2.b])
        tr(psum[1].ap[:, 0:4], gin2.ap[0:4, 0:128], ident_f[0:4, 0:4], [gin2.b, cst_f.b], [psum[1].b])
        for l in range(DEPTH):
            S.add("dve", lambda e, l=l: e.tensor_scalar(out=gsub.ap[:, l:l + 1], in0=psum[1].ap[:, l:l + 1],
                                                        scalar1=float(1.0 - lambda_init(l)), scalar2=None,
                                                        op0=ALU.mult), [psum[1].b], [gsub.b])
        lamt = [rot(tf, "tf") for _ in range(2)]
        lam_sb = lamt[0]
        dma("sp", lam_sb.ap, lam_d[0:1, 0:512].partition_broadcast(128), [], [lam_sb.b])
        dma("sp", lamt[1].ap, lam_d[0:1, 512:1024].partition_broadcast(128), [], [lamt[1].b])
        lprod = rot(tf, "tf")
        lsum = rot(tf, "tf")
        for l in range(DEPTH):
            src = lamt[l // 2].ap[:, (l % 2) * 256:(l % 2) * 256 + 256].rearrange("p (a b) -> p a b", a=4)
            for j in range(2):
                S.add("dve", lambda e, src=src, j=j, l=l: e.tensor_tensor(
                    out=lprod.ap[:, (2 * l + j) * 64:(2 * l + j + 1) * 64], in0=src[:, 2 * j, :],
                    in1=src[:, 2 * j + 1, :], op=ALU.mult), [lamt[l // 2].b], [lprod.b])
        S.add("dve", lambda e: e.tensor_reduce(out=lsum.ap[:, 0:8],
                                               in_=lprod.ap[:, 0:512].rearrange("p (a b) -> p a b", a=8),
                                               axis=AX.X, op=ALU.add), [lprod.b], [lsum.b])
        act(lsum.ap[:, 8:16], lsum.ap[:, 0:8], AF.Exp, [lsum.b], [lsum.b])
        for l in range(DEPTH):
            S.add("dve", lambda e, l=l: e.scalar_tensor_tensor(
                out=neglam.ap[:, l:l + 1], in0=lsum.ap[:, 8 + 2 * l + 1:8 + 2 * l + 2],
                scalar=float(-lambda_init(l)), in1=lsum.ap[:, 8 + 2 * l:8 + 2 * l + 1],
                op0=ALU.add, op1=ALU.subtract), [lsum.b], [neglam.b])

        def cast_w(dst, src, l, key, b):
            d2 = dst[l].rearrange("r (a b) -> (r a) b", b=b)
            s2 = src[l].rearrange("r (a b) -> (r a) b", b=b)
            dma("pool", d2, s2, [], [WB_b[key][l]]).nobar = True

        for l in range(n_layers):
            cast_w(wb_in, w_in_d, l, "in", 2048)
            cast_w(wb_a, w_a_d, l, "a", 1024)
            cast_w(wb_b, w_b_d, l, "b", 1024)
            cast_w(wb_out, w_out_d, l, "out", 1024)
            cast_w(wb_gu, w_gu_d, l, "gu", 1408)
            cast_w(wb_dn, w_dn_d, l, "dn", 1024)

        def layout_AC():
            ovA.reset()
            L = {}
            L["xbuf"] = [ovA.alloc([128, 8, TC], F32, nsub=8) for _ in range(2)]
            L["sqb"] = ovA.alloc([128, 8, TC], BF16, nsub=8)
            L["wbig"] = [ovA.alloc([128, 8, TC], BF16) for _ in range(4)]
            L["oac"] = ovA.alloc([128, 4, TC], BF16)
            L["obc"] = ovA.alloc([128, 4, TC], BF16)
            L["merged"] = ovA.alloc([128, 8, TC], BF16, nsub=8)
            L["M"] = ovA.alloc([128, 8, TC], F32, nsub=8)
            L["H2"] = ovA.alloc([128, 8, TC], BF16, nsub=8)
            L["ACT"] = ovA.alloc([128, NFC, TC], BF16, nsub=NFC)
            return L

        def layout_B():
            ovA.reset()
            L = {}
            L["cos"] = ovA.alloc([128, S_], F32)
            L["sin"] = ovA.alloc([128, S_], F32)
            L["QT"] = ovA.alloc([128, S_], BF16, nsub=NTC)
            L["KT"] = ovA.alloc([128, S_], BF16, nsub=NTC)
            L["VT"] = ovA.alloc([128, S_], BF16, nsub=NTC)
            L["VD"] = ovA.alloc([128, 32, 128], BF16, nsub=8)
            L["VB"] = ovA.alloc([128, 32, 2, 128], BF16, nsub=8)
            L["PT"] = [ovA.alloc([128, TC], BF16) for _ in range(4)]
            L["wsm"] = [ovA.alloc([128, 8, 128], BF16) for _ in range(6)]
            L["ACC"] = ovA.alloc([128, 2, S_], F32, nsub=2)
            L["DACC"] = [ovA.alloc([128, TC], F32) for _ in range(4)]
            return L

        LA = layout_AC()
        wi = {"n": 0}

        def x_to_xt():
            for tc in range(NTC):
                xin = LA["xbuf"][tc % 2]
                xin_ap = xin.ap.rearrange("p a b -> p (a b)").rearrange("p (a d) -> p a d", a=4)
                dma("sp", xin_ap, x_d[tc * TC:(tc + 1) * TC, :].rearrange("(a p) d -> p a d", p=128),
                    [], xin.s)
                st = LA["M"]
                for c in range(8):
                    ps = psum[c % 4]
                    for a in range(4):
                        tr(ps.ap[:, a * 128:(a + 1) * 128], xin_ap[:, a, c * 128:(c + 1) * 128], ident_f,
                           xin.s + [cst_f.b], [ps.b])
                    cp("dve" if c % 2 == 0 else "act", st.ap[:, c, :], ps.ap, [ps.b], st.s)
                dma("pool", XT_d[:, :, tc * TC:(tc + 1) * TC], st.ap, st.s, [XT_b[tc]])

        x_to_xt()

        def phase_A(l):
            for tc in range(NTC):
                tsl = slice(tc * TC, (tc + 1) * TC)
                xc = LA["xbuf"][tc % 2]
                dma("sp", xc.ap, XT_d[:, :, tsl], [XT_b[tc]], xc.s)
                sq = LA["sqb"]
                act(sq.ap, xc.ap, AF.Square, xc.s, sq.s)
                ps = psum[tc % 2]
                for c in range(8):
                    mm(ps.ap, ones_bf, sq.ap[:, c, :], c == 0, c == 7, sq.s + [cst_bf.b], [ps.b])
                rs = rot(rsr, "rs")
                rstd_from_ssq(ps, 1.0 / D_, NORM_EPS, rs)
                hb = rot(hbuf, "hb")
                for c in range(8):
                    stt(hb.ap[:, c, :], xc.ap[:, c, :], gain(0, l, c), rs.ap, ALU.mult, ALU.mult,
                        xc.s + [rs.b, gains.b], [hb.b])
                dma("pool", HT_d[:, :, tsl], hb.ap, [hb.b], [HT_b[tc]])

        def load_wsm(LB, l, col0):
            w = LB["wsm"][wi["n"] % 6]
            wi["n"] += 1
            dma("sp", w.ap, wb_in[l, :, col0:col0 + 128].rearrange("(k p) n -> p k n", p=128),
                [WB_b["in"][l]], [w.b])
            return w

        def project(LB, l, cols, dsts, ropes):
            ws = [load_wsm(LB, l, c0) for c0 in cols]
            pending = []

            def tail(ps, qb, dst, tc, k):
                tsl = slice(tc * TC, (tc + 1) * TC)
                ps2 = psum[3 + k % 2]
                mm(ps2.ap, perm_bf, qb.ap, True, True, [qb.b, cst_bf.b], [ps2.b])
                t1 = rot(tf, "tf")
                t2 = rot(tf, "tf")
                tt("dve", t1.ap, ps.ap, LB["cos"].ap[:, tsl], ALU.mult, [ps.b, LB["cos"].b, qb.b], [t1.b])
                tt("dve", t2.ap, ps2.ap, LB["sin"].ap[:, tsl], ALU.mult, [ps2.b, LB["sin"].b], [t2.b])
                tt("pool", dst.ap[:, tsl], t1.ap, t2.ap, ALU.add, [t1.b, t2.b], [dst.s[tc]])

            k = 0
            for tc in range(NTC):
                tsl = slice(tc * TC, (tc + 1) * TC)
                hc = rot(hbuf, "hb")
                dma("sp", hc.ap, HT_d[:, :, tsl], [HT_b[tc]], [hc.b])
                for j, (w, dst, rope) in enumerate(zip(ws, dsts, ropes)):
                    ps = psum[j % 3]
                    for kc in range(8):
                        mm(ps.ap, w.ap[:, kc, :], hc.ap[:, kc, :], kc == 0, kc == 7, [w.b, hc.b], [ps.b])
                    if rope:
                        qb = rot(tb, "tb")
                        cp("act", qb.ap, ps.ap, [ps.b], [qb.b])
                        new_p = (ps, qb, dst, tc, k)
                        k += 1
                    else:
                        cp("act", dst.ap[:, tsl], ps.ap, [ps.b], [dst.s[tc]])
                        new_p = None
                    while pending:
                        tail(*pending.pop(0))
                    if new_p is not None:
                        pending.append(new_p)
            while pending:
                tail(*pending.pop(0))

        def v_transposes(LB, d, dil_mode):
            VT = LB["VT"]
            nkc = 32 // d
            psb = [psum[5].ap.bitcast(BF16), psum[6].ap.bitcast(BF16)]
            for g4 in range(8):
                ps = psum[5 + g4 % 2]
                pv = psb[g4 % 2]
                for a in range(4):
                    ch = g4 * 4 + a
                    r, c = ch // nkc, ch % nkc
                    t0 = r + d * 128 * c
                    src = VT.ap[:, t0:t0 + 127 * d + 1:d] if d > 1 else VT.ap[:, t0:t0 + 128]
                    tr(pv[:, a * 128:(a + 1) * 128], src, ident_bf, VT.s + [cst_bf.b], [ps.b])
                if dil_mode:
                    VB = LB["VB"]
                    cp("dve", VB.ap[:, g4 * 4:(g4 + 1) * 4, :, 0:64],
                       pv[:, 0:512].rearrange("p (a h e) -> p a h e", a=4, h=2), [ps.b], [VB.s[g4]])
                else:
                    VD = LB["VD"]
                    cp("dve", VD.ap[:, g4 * 4:(g4 + 1) * 4, :],
                       pv[:, 0:512].rearrange("p (a e) -> p a e", a=4), [ps.b], [VD.s[g4]])

        def diff_head(LB, l, hd):
            QT, KT, VD = LB["QT"], LB["KT"], LB["VD"]
            project(LB, l, [128 * hd, 512 + 128 * hd, 1024 + 128 * hd], [QT, KT, LB["VT"]], [True, True, False])
            if upto == "B1":
                return
            v_transposes(LB, 1, False)
            if upto == "B2":
                return
            st_ring = [psum[0], psum[1], psum[2], psum[3]]
            num = [psum[4], psum[5]]
            den = [psum[6], psum[7]]
            ssq_ps = psum[7]
            PT = LB["PT"]
            DACC = LB["DACC"]
            ones_f = cst_f.ap[:, 512:640]
            pairs = [(qc, kc) for qc in range(NTC) for kc in range(32)]
            npairs = len(pairs)

            def QKp(p):
                qc, kc = pairs[p]
                for c in range(2):
                    st = st_ring[(2 * p + c) % 4]
                    mm(st.ap, KT.ap[64 * c:64 * c + 64, kc * 128:(kc + 1) * 128],
                       QT.ap[64 * c:64 * c + 64, qc * TC:(qc + 1) * TC], True, True, [KT.s[kc // 4], QT.s[qc]], [st.b])
                for c in range(2):
                    st = st_ring[(2 * p + c) % 4]
                    pt = PT[(2 * p + c) % 4]
                    act(pt.ap, st.ap, AF.Exp, [st.b], [pt.b], scale=0.125)

            def PVp(p):
                qc, kc = pairs[p]
                for c in range(2):
                    pt = PT[(2 * p + c) % 4]
                    mm(num[c].ap, VD.ap[:, kc, :], pt.ap, kc == 0, kc == 31, [VD.s[kc // 4], pt.b], [num[c].b])
                    acc = DACC[2 * (qc % 2) + c]
                    if kc == 0:
                        cp("dve", acc.ap, pt.ap, [pt.b], [acc.b])
                    else:
                        tt("dve", acc.ap, acc.ap, pt.ap, ALU.add, [acc.b, pt.b], [acc.b])

            keep = {}

            def fin_num(qc):
                nm = [rot(tf, "tf") for _ in range(2)]
                for c in range(2):
                    cp("dve", nm[c].ap, num[c].ap, [num[c].b], [nm[c].b])
                keep[qc] = [nm]

            def fin_a(qc):
                nm = keep[qc][0]
                dn = [rot(tf, "tf") for _ in range(2)]
                for c in range(2):
                    acc = DACC[2 * (qc % 2) + c]
                    mm(den[c].ap, ones_f, acc.ap, True, True, [acc.b, cst_f.b], [den[c].b])
                for c in range(2):
                    recip(dn[c].ap, den[c].ap, [den[c].b], [dn[c].b])
                for c in range(2):
                    tt("dve", nm[c].ap, nm[c].ap, dn[c].ap, ALU.mult, [nm[c].b, dn[c].b], [nm[c].b])
                a_ = rot(tf, "tf")
                stt(a_.ap, nm[1].ap, neglam.ap[:, l:l + 1], nm[0].ap, ALU.mult, ALU.add,
                    [nm[0].b, nm[1].b, neglam.b], [a_.b])
                sq = rot(tb, "tb")
                tt("pool", sq.ap, a_.ap, a_.ap, ALU.mult, [a_.b], [sq.b])
                keep[qc] = (a_, sq)

            def fin_b(qc):
                a_, sq = keep.pop(qc)
                mm(ssq_ps.ap, ones_bf, sq.ap, True, True, [sq.b, cst_bf.b], [ssq_ps.b])
                rs = rot(rsr, "rs")
                rstd_from_ssq(ssq_ps, 1.0 / 128, SUBLN_EPS, rs)
                ob = rot(obuf, "ob")
                stt(ob.ap, a_.ap, gsub.ap[:, l:l + 1], rs.ap, ALU.mult, ALU.mult, [a_.b, rs.b, gsub.b], [ob.b])
                dma("pool", OA_d[:, hd, qc * TC:(qc + 1) * TC], ob.ap, [ob.b], [OA_b[hd][qc]])

            pend = []
            QKp(0)
            for p in range(npairs):
                if p + 1 < npairs:
                    QKp(p + 1)
                PVp(p)
                qc, kc = pairs[p]
                if kc == 31:
                    fin_num(qc)
                    pend.append((p + 2, fin_a, qc))
                    pend.append((p + 12, fin_b, qc))
                    pend.sort(key=lambda t: t[0])
                while pend and pend[0][0] <= p:
                    _, f, q_ = pend.pop(0)
                    f(q_)
            while pend:
                _, f, q_ = pend.pop(0)
                f(q_)

        def dil_pair(LB, l, j):
            QT, KT, VB, ACC = LB["QT"], LB["KT"], LB["VB"], LB["ACC"]
            wstate = {"n": 0}
            for g, d in enumerate(DIL):
                base = 1536 + 1536 * g + 128 * j
                project(LB, l, [base, base + 512, base + 1024], [QT, KT, LB["VT"]], [True, True, False])
                v_transposes(LB, d, True)
                Lg = S_ // d
                nkc = Lg // 128
                items = [(h, r, c) for h in range(2) for r in range(d) for c in range(nkc)]
                n = len(items)
                LA = 2
                PT = LB["PT"]
                st_ring = [psum[0], psum[1], psum[2]]
                wps = [None, None]

                def toks(r, p0, p1, d=d):
                    return slice(r + d * p0, r + d * (p1 - 1) + 1, d) if d > 1 else slice(p0, p1)

                def geom(c, Lg=Lg):
                    q0 = max(0, 128 * c - 64)
                    q1 = min(Lg, 128 * c + 192)
                    return q0, q1

                def QK(i):
                    h, r, c = items[i]
                    rows = slice(64 * h, 64 * h + 64)
                    q0, q1 = geom(c)
                    nq = q1 - q0
                    moff = q0 - (128 * c - 64)
                    st = st_ring[i % 3]
                    mm(st.ap[:, 0:nq], KT.ap[rows, toks(r, 128 * c, 128 * c + 128)], QT.ap[rows, toks(r, q0, q1)],
                       True, False, KT.s + QT.s, [st.b])
                    mm(st.ap[:, 0:nq], ident_bf, mask_bf[:, moff:moff + nq], False, True, [cst_bf.b], [st.b])
                    pt = PT[i % 4]
                    act(pt.ap[:, 0:nq], st.ap[:, 0:nq], AF.Exp, [st.b], [pt.b], scale=0.125)

                def PV(i):
                    h, r, c = items[i]
                    q0, q1 = geom(c)
                    nq = q1 - q0
                    pt = PT[i % 4]
                    ch = r * nkc + c
                    va = VB.ap[:, ch, h, :]
                    split = 128 * c + 64 - q0
                    for w, (c0, c1) in ((c, (0, split)), (c + 1, (split, nq))):
                        first = (w == c + 1) or (c == 0)
                        last = (w == c) or (c == nkc - 1)
                        if first:
                            wps[w % 2] = psum[3 + wstate["n"] % 4]
                            wstate["n"] += 1
                        wp = wps[w % 2]
                        mm(wp.ap[:, 0:c1 - c0], va, pt.ap[:, c0:c1], first, last, [VB.s[ch // 4], pt.b], [wp.b])
                        if last:
                            p0 = max(0, 128 * w - 64)
                            p1 = min(Lg, 128 * w + 64)
                            assert p1 - p0 == c1 - c0, (p0, p1, c0, c1)
                            dst = ACC.ap[:, h, toks(r, p0, p1)]
                            if g == 0:
                                cp("dve", dst, wp.ap[:, 0:p1 - p0], [wp.b], [ACC.s[h]])
                            else:
                                tt("dve", dst, wp.ap[:, 0:p1 - p0], dst, ALU.add, [wp.b, ACC.s[h]], [ACC.s[h]])

                for i in range(min(LA, n)):
                    QK(i)
                for i in range(n):
                    if i + LA < n:
                        QK(i + LA)
                    PV(i)
            for tc in range(NTC):
                tsl = slice(tc * TC, (tc + 1) * TC)
                ob = rot(obuf, "ob")
                for h in range(2):
                    l1 = rot(tf, "tf")
                    act(l1.ap[64:128, :], ACC.ap[64:128, h, tsl], AF.Ln, [ACC.s[h]], [l1.b])
                    r1 = rot(tf, "tf")
                    act(r1.ap[64:128, :], l1.ap[64:128, :], AF.Exp, [l1.b], [r1.b], scale=-1.0)
                    rd = rot(tf, "tf")
                    cp("dve", rd.ap[0:64, :], r1.ap[64:128, :], [r1.b], [rd.b])
                    tt("dve", ob.ap[64 * h:64 * h + 64, :], ACC.ap[0:64, h, tsl], rd.ap[0:64, :], ALU.mult,
                       [ACC.s[h], rd.b], [ob.b])
                dma("pool", OB_d[:, j, tsl], ob.ap, [ob.b], [OB_b[j][tc]])

        def phase_B(l, LB):
            for hd in range(nheads[0]):
                mark("diff%d.%d" % (l, hd))
                diff_head(LB, l, hd)
            for j in range(nheads[1]):
                mark("dil%d.%d" % (l, j))
                dil_pair(LB, l, j)

        def load_wbig(src_ap, kcs, buf):
            w = LA["wbig"][wi["n"] % 4]
            wi["n"] += 1
            dma("sp", w.ap[:, 0:kcs, :], src_ap.rearrange("(k p) n -> p k n", p=128), [buf], [w.b])
            return w

        def sq_chunk(M, j, ssq_ps):
            sq = LA["sqb"]
            act(sq.ap[:, j, :], M.ap[:, j, :], AF.Square, [M.s[j]], [sq.s[j]])
            mm(ssq_ps.ap, ones_bf, sq.ap[:, j, :], j == 0, j == 7, [sq.s[j], cst_bf.b], [ssq_ps.b])

        def norm_residual(l, ty, xc, ssq_ps, after_chunk=None):
            M = LA["M"]
            rs = rot(rsr, "rs")
            rstd_from_ssq(ssq_ps, 1.0 / D_, NORM_EPS, rs)
            for c in range(8):
                t = rot(tf, "tf")
                stt(t.ap, M.ap[:, c, :], gain(ty, l, c), rs.ap, ALU.mult, ALU.mult, [M.s[c], rs.b, gains.b], [t.b])
                tt("dve", xc.ap[:, c, :], xc.ap[:, c, :], t.ap, ALU.add,
                   [xc.s[c], t.b], [xc.s[c]])
                if after_chunk is not None:
                    after_chunk(c)

        def phase_C(l):
            pr = {"n": 0}

            def nps():
                p = psum[pr["n"] % 6]
                pr["n"] += 1
                return p

            for tc in range(NTC):
                tsl = slice(tc * TC, (tc + 1) * TC)
                xc = LA["xbuf"][tc % 2]
                dma("sp", xc.ap, XT_d[:, :, tsl], [XT_b[tc]], xc.s)
                hc = rot(hbuf, "hb")
                dma("sp", hc.ap, HT_d[:, :, tsl], [HT_b[tc]], [hc.b])
                oac, obc = LA["oac"], LA["obc"]
                dma("sp", oac.ap, OA_d[:, :, tsl], [OA_b[h][tc] for h in range(4)], [oac.b])
                dma("sp", obc.ap, OB_d[:, :, tsl], [OB_b[h][tc] for h in range(4)], [obc.b])
                merged = LA["merged"]
                for half in range(2):
                    wgA = load_wbig(wb_in[l, :, 6144 + 512 * half:6144 + 512 * half + 512], 8, WB_b["in"][l])
                    wgB = load_wbig(wb_in[l, :, 7168 + 512 * half:7168 + 512 * half + 512], 8, WB_b["in"][l])
                    wa = load_wbig(wb_a[l, :, 512 * half:512 * half + 512], 4, WB_b["a"][l])
                    wb_ = load_wbig(wb_b[l, :, 512 * half:512 * half + 512], 4, WB_b["b"][l])
                    for jj in range(4):
                        j = 4 * half + jj
                        js = slice(128 * jj, 128 * jj + 128)
                        parts = []
                        for (wg, wp_, oc) in ((wgA, wa, oac), (wgB, wb_, obc)):
                            pg = nps()
                            for kc in range(8):
                                mm(pg.ap, wg.ap[:, kc, js], hc.ap[:, kc, :], kc == 0, kc == 7, [wg.b, hc.b], [pg.b])
                            sg = rot(tf, "tf")
                            act(sg.ap, pg.ap, AF.Sigmoid, [pg.b], [sg.b])
                            pp = nps()
                            for kc in range(4):
                                mm(pp.ap, wp_.ap[:, kc, js], oc.ap[:, kc, :], kc == 0, kc == 3, [wp_.b, oc.b], [pp.b])
                            t = rot(tf, "tf")
                            tt("dve", t.ap, pp.ap, sg.ap, ALU.mult, [pp.b, sg.b], [t.b])
                            parts.append(t)
                        tt("pool", merged.ap[:, j, :], parts[0].ap, parts[1].ap, ALU.add,
                           [parts[0].b, parts[1].b], [merged.s[j]])
                M = LA["M"]
                ssq1, ssq2 = psum[7], psum[6]
                for half in range(2):
                    wo = load_wbig(wb_out[l, :, 512 * half:512 * half + 512], 8, WB_b["out"][l])
                    for jj in range(4):
                        j = 4 * half + jj
                        pm = nps()
                        for kc in range(8):
                            mm(pm.ap, wo.ap[:, kc, 128 * jj:128 * jj + 128], merged.ap[:, kc, :], kc == 0, kc == 7,
                               [wo.b] + merged.s, [pm.b])
                        cp("act" if j % 2 else "dve", M.ap[:, j, :], pm.ap, [pm.b], [M.s[j]])
                        if j >= 1:
                            sq_chunk(M, j - 1, ssq1)
                sq_chunk(M, 7, ssq1)
                if upto == "C0b":
                    dma("pool", XT_d[:, :, tsl], M.ap, M.s, [XT_b[tc]])
                    continue
                norm_residual(l, 1, xc, ssq1, after_chunk=lambda c: sq_chunk(xc, c, ssq2))
                if upto == "C1":
                    dma("pool", XT_d[:, :, tsl], xc.ap, xc.s, [XT_b[tc]])
                    continue
                rs = rot(rsr, "rs")
                rstd_from_ssq(ssq2, 1.0 / D_, NORM_EPS, rs)
                H2 = LA["H2"]
                for c in range(8):
                    stt(H2.ap[:, c, :], xc.ap[:, c, :], gain(2, l, c), rs.ap, ALU.mult, ALU.mult,
                        [xc.s[c], rs.b, gains.b], [H2.s[c]])
                ACTT = LA["ACT"]
                for i4 in range(0, NFC, 4):
                    n4 = min(4, NFC - i4)
                    wg = LA["wbig"][wi["n"] % 4]
                    wi["n"] += 1
                    dma("sp", wg.ap[:, :, 0:128 * n4],
                        wb_gu[l, :, 128 * i4:128 * (i4 + n4)].rearrange("(k p) n -> p k n", p=128),
                        [WB_b["gu"][l]], [wg.b])
                    wu = LA["wbig"][wi["n"] % 4]
                    wi["n"] += 1
                    dma("sp", wu.ap[:, :, 0:128 * n4],
                        wb_gu[l, :, FF + 128 * i4:FF + 128 * (i4 + n4)].rearrange("(k p) n -> p k n", p=128),
                        [WB_b["gu"][l]], [wu.b])
                    for ii in range(n4):
                        i = i4 + ii
                        isl = slice(128 * ii, 128 * ii + 128)
                        pg = nps()
                        for kc in range(8):
                            mm(pg.ap, wg.ap[:, kc, isl], H2.ap[:, kc, :], kc == 0, kc == 7, [wg.b] + H2.s, [pg.b])
                        pu = nps()
                        for kc in range(8):
                            mm(pu.ap, wu.ap[:, kc, isl], H2.ap[:, kc, :], kc == 0, kc == 7, [wu.b] + H2.s, [pu.b])
                        sg = rot(tf, "tf")
                        act(sg.ap, pg.ap, AF.Silu, [pg.b], [sg.b])
                        tt("dve", ACTT.ap[:, i, :], pu.ap, sg.ap, ALU.mult, [pu.b, sg.b], [ACTT.s[i]])
                for half in range(2):
                    pms = [nps() for _ in range(4)]
                    for (k0, kn) in ((0, 8), (8, 8), (16, 6)):
                        wd = load_wbig(wb_dn[l, 128 * k0:128 * (k0 + kn), 512 * half:512 * half + 512], kn,
                                       WB_b["dn"][l])
                        for jj in range(4):
                            for kk in range(kn):
                                kc = k0 + kk
                                mm(pms[jj].ap, wd.ap[:, kk, 128 * jj:128 * jj + 128], ACTT.ap[:, kc, :],
                                   kc == 0, kc == NFC - 1, [wd.b, ACTT.s[kc]], [pms[jj].b])
                        if half == 1 and k0 == 0:
                            for jj in range(4):
                                sq_chunk(M, jj, ssq1)
                    for jj in range(4):
                        j = 4 * half + jj
                        cp("act" if j % 2 else "dve", M.ap[:, j, :], pms[jj].ap, [pms[jj].b], [M.s[j]])
                for jj in range(4, 8):
                    sq_chunk(M, jj, ssq1)
                norm_residual(l, 3, xc, ssq1)
                dma("pool", XT_d[:, :, tsl], xc.ap, xc.s, [XT_b[tc]])

        def xt_to_out():
            for tc in range(NTC):
                tsl = slice(tc * TC, (tc + 1) * TC)
                xc = LA["xbuf"][tc % 2]
                dma("sp", xc.ap, XT_d[:, :, tsl], [XT_b[tc]], xc.s)
                st = LA["M"]
                st_ap = st.ap.rearrange("p a b -> p (a b)").rearrange("p (a d) -> p a d", a=4)
                for a in range(4):
                    for half in range(2):
                        ps = psum[(2 * a + half) % 4]
                        for cc in range(4):
                            c = 4 * half + cc
                            tr(ps.ap[:, cc * 128:(cc + 1) * 128], xc.ap[:, c, a * 128:(a + 1) * 128], ident_f,
                               xc.s + [cst_f.b], [ps.b])
                        cp("dve" if half else "act", st_ap[:, a, 512 * half:512 * half + 512], ps.ap, [ps.b], st.s)
                dma("pool", out_d[tc * TC:(tc + 1) * TC, :].rearrange("(a p) d -> p a d", p=128), st_ap,
                    st.s, [out_bs[tc]])

        S.marks = []
        mark = lambda nm: S.marks.append((nm, len(S.prog["pe"])))
        for l in range(n_layers):
            if upto == "prologue":
                break
            mark("A%d" % l)
            phase_A(l)
            if upto == "A":
                break
            S.barrier()
            LB = layout_B()
            dma("sp", LB["cos"].ap, cos_d, [], [LB["cos"].b])
            dma("sp", LB["sin"].ap, sin_d, [], [LB["sin"].b])
            for ch in range(32):
                for h in range(2):
                    S.add("pool", lambda e, ch=ch, h=h, vb=LB["VB"]: e.memset(vb.ap[:, ch, h, 64:128], 1.0),
                          [], [LB["VB"].s[ch // 4]])
            if upto == "B0":
                break
            phase_B(l, LB)
            if upto in ("B1", "B2"):
                break
            S.barrier()
            LA = layout_AC()
            if upto == "B":
                break
            mark("C%d" % l)
            phase_C(l)
        mark("epi")
        xt_to_out()
        mark("end")
        build.marks = S.marks
        if dbg:
            for name, shape, dt in dbg:
                src = {"XT": XT_d, "HT": HT_d, "OAT": OA_d, "OBT": OB_d}[name]
                dma("sp", dbg_d[name], src, sum([XT_b, HT_b] + OA_b + OB_b, []), [out_bs[NTC + len(dbg_d) % 4]])
        S.add("sp", None, out_bs, [])
        S.emit(nc, block, sems, rings)
    return nc


def _rope_tables():
    inv = (1.0 / (np.float32(10000.0) ** (np.arange(0, 64, 2, dtype=np.float32) / np.float32(64)))).astype(np.float32)
    ang = (np.arange(S_, dtype=np.float32)[:, None] * inv[None, :]).astype(np.float32)
    cos = np.cos(ang).astype(np.float32).T
    sin = np.sin(ang).astype(np.float32).T
    cos2 = np.concatenate([cos, cos, cos, cos], axis=0)
    ssin = np.concatenate([-sin, sin, -sin, sin], axis=0)
    return np.ascontiguousarray(cos2), np.ascontiguousarray(ssin)


def _consts():
    c = np.zeros((128, 640), np.float32)
    c[:, 0:128] = np.eye(128, dtype=np.float32)
    for m in range(128):
        k = m + 32 if (m % 64) < 32 else m - 32
        c[k, 128 + m] = 1.0
    i = np.arange(128)[:, None]
    jj = np.arange(256)[None, :]
    c[:, 256:512] = np.where((jj >= i) & (jj <= i + 128), 0.0, MASK_NEG)
    c[:, 512:640] = 1.0
    return c


_NC_CACHE = {}


def make_inputs(x, w_in, w_a, w_b, w_out, lam, subln_g, norm_mix_pre, norm_mix_post,
                w_gate_up, w_down, norm_ffn_pre, norm_ffn_post):
    f = lambda a: np.ascontiguousarray(np.asarray(a, dtype=np.float32))
    cos2, ssin = _rope_tables()
    norms = np.stack([f(norm_mix_pre), f(norm_mix_post), f(norm_ffn_pre), f(norm_ffn_post)], 0)
    norms = np.ascontiguousarray(norms.reshape(4 * DEPTH * 8, 128))
    shared = {
        "w_in": f(w_in), "w_a": f(w_a), "w_b": f(w_b), "w_out": f(w_out),
        "lam": f(lam).reshape(1, -1), "subln_g": f(subln_g), "norms": norms,
        "w_gate_up": f(w_gate_up), "w_down": f(w_down),
        "cos2": cos2, "ssin": ssin, "consts": _consts(),
    }
    x = f(x)
    return [dict(shared, x=x[i]) for i in range(N_CORES)]


def kernel(**inputs):
    in_maps = make_inputs(**inputs)
    if "nc" not in _NC_CACHE:
        _NC_CACHE["nc"] = build()
    res = run_bass_kernel_spmd(_NC_CACHE["nc"], in_maps, core_ids=list(range(N_CORES)))
    return np.stack([np.asarray(r["out"], dtype=np.float32) for r in res.results], axis=0)
```
